# Optimizing a Trainium2 kernel written in Bass

```python
import math
import jax, jax.numpy as jnp
from jax import lax
import numpy as np

D_MODEL = 2048
BATCH = 4
SEQ = 4096
DEPTH = 2

MIX_WIDTH = D_MODEL
ATTN_WIDTH = MIX_WIDTH // 2
SSM_WIDTH = MIX_WIDTH - ATTN_WIDTH
ATTN_HEAD_DIM = 128
ATTN_HEADS = ATTN_WIDTH // ATTN_HEAD_DIM
MOBA_BLOCK = 256
MOBA_TOPK = 3
Q_CHUNK = 32
SSM_HEAD_DIM = 64
SSM_HEADS = SSM_WIDTH // SSM_HEAD_DIM
SSM_GROUPS = 2
SSM_HEADS_PER_GROUP = SSM_HEADS // SSM_GROUPS
SSM_STATE = 128
SSM_CONV = 4
SSD_CHUNK = 128
XBC_WIDTH = SSM_WIDTH + 2 * SSM_GROUPS * SSM_STATE
IN_COLS = 3 * ATTN_WIDTH + SSM_WIDTH + XBC_WIDTH + SSM_HEADS
SPLIT_POINTS = (ATTN_WIDTH, 2 * ATTN_WIDTH, 3 * ATTN_WIDTH,
                3 * ATTN_WIDTH + SSM_WIDTH, 3 * ATTN_WIDTH + SSM_WIDTH + XBC_WIDTH)
D_FF = 5632
FFN_CONV = 3
EPS = 1e-6

kernel_name = "hymba_moba_ssd_convglu"


def rms_norm(x, w):
    xf = x.astype(jnp.float32)
    y = xf * lax.rsqrt(jnp.mean(xf * xf, axis=-1, keepdims=True) + EPS)
    return (y * w.astype(jnp.float32)).astype(x.dtype)


def causal_dwconv(u, w, bias):
    k_width = w.shape[0]
    s = u.shape[1]
    up = jnp.pad(u, ((0, 0), (k_width - 1, 0), (0, 0)))
    out = bias + up[:, 0:s] * w[0]
    for j in range(1, k_width):
        out = out + up[:, j:j + s] * w[j]
    return out


def moba_attention(q, k, v):
    b, h, s, dh = q.shape
    nb = -(-s // MOBA_BLOCK)
    pad = nb * MOBA_BLOCK - s
    kb = jnp.pad(k, ((0, 0), (0, 0), (0, pad), (0, 0))).reshape(b, h, nb, MOBA_BLOCK, dh)
    vb = jnp.pad(v, ((0, 0), (0, 0), (0, pad), (0, 0))).reshape(b, h, nb, MOBA_BLOCK, dh)
    k_mean = jnp.mean(kb, axis=3)
    n_sel = min(MOBA_TOPK, nb - 1)
    scale = dh ** -0.5
    blk_ids = jnp.arange(nb, dtype=jnp.int32)
    key_off = jnp.arange(MOBA_BLOCK, dtype=jnp.int32)
    gather_blocks = jax.vmap(jax.vmap(lambda blocks, idx: blocks[idx]))
    nq = s // Q_CHUNK
    q_chunks = q.reshape(b, h, nq, Q_CHUNK, dh).transpose(2, 0, 1, 3, 4)

    def chunk_attend(args):
        qc, c = args
        q_pos = c * Q_CHUNK + jnp.arange(Q_CHUNK, dtype=jnp.int32)
        own = (c * Q_CHUNK) // MOBA_BLOCK
        k_own = lax.dynamic_index_in_dim(kb, own, axis=2, keepdims=False)
        v_own = lax.dynamic_index_in_dim(vb, own, axis=2, keepdims=False)
        k_pos = own * MOBA_BLOCK + key_off
        s_own = jnp.einsum('bhqd,bhkd->bhqk', qc, k_own).astype(jnp.float32) * scale
        s_own = jnp.where(k_pos[None, :] <= q_pos[:, None], s_own, -jnp.inf)
        if n_sel == 0:
            p_own = jax.nn.softmax(s_own, axis=-1).astype(v.dtype)
            return jnp.einsum('bhqk,bhkd->bhqd', p_own, v_own)
        gate = jnp.einsum('bhqd,bhnd->bhqn', qc, k_mean).astype(jnp.float32)
        gate = jnp.where(blk_ids < own, gate, -jnp.inf)
        _, sel = lax.top_k(gate, n_sel)
        valid = sel < own
        k_sel = gather_blocks(kb, sel)
        v_sel = gather_blocks(vb, sel)
        s_sel = jnp.einsum('bhqd,bhqjkd->bhqjk', qc, k_sel).astype(jnp.float32) * scale
        s_sel = jnp.where(valid[..., None], s_sel, -jnp.inf).reshape(b, h, Q_CHUNK, n_sel * MOBA_BLOCK)
        p = jax.nn.softmax(jnp.concatenate([s_sel, s_own], axis=-1), axis=-1).astype(v.dtype)
        p_sel = p[..., :n_sel * MOBA_BLOCK].reshape(b, h, Q_CHUNK, n_sel, MOBA_BLOCK)
        p_own = p[..., n_sel * MOBA_BLOCK:]
        return (jnp.einsum('bhqjk,bhqjkd->bhqd', p_sel, v_sel)
                + jnp.einsum('bhqk,bhkd->bhqd', p_own, v_own))

    out = lax.map(chunk_attend, (q_chunks, jnp.arange(nq, dtype=jnp.int32)))
    return out.transpose(1, 2, 0, 3, 4).reshape(b, h, s, dh)


def ssd_chunked_scan(xs, dt, a, bm, cm):
    b, s, g, r, p = xs.shape
    n = bm.shape[-1]
    nc = s // SSD_CHUNK
    L = SSD_CHUNK
    log_a = (dt * a).reshape(b, nc, L, g, r).transpose(0, 3, 4, 1, 2)
    xdt = (xs * dt[..., None]).reshape(b, nc, L, g, r, p)
    bc = bm.reshape(b, nc, L, g, n)
    cc = cm.reshape(b, nc, L, g, n)
    a_cum = jnp.cumsum(log_a, axis=-1)
    tril = jnp.tril(jnp.ones((L, L), dtype=bool))
    seg = a_cum[..., :, None] - a_cum[..., None, :]
    decay = jnp.where(tril, jnp.exp(jnp.where(tril, seg, 0.0)), 0.0)
    cb = jnp.einsum('bclgn,bcsgn->bgcls', cc, bc)
    y_diag = jnp.einsum('bgrcls,bcsgrp->bclgrp', cb[:, :, None] * decay, xdt)
    decay_to_end = jnp.exp(a_cum[..., -1:] - a_cum)
    chunk_states = jnp.einsum('bclgn,bgrcl,bclgrp->cbgrpn', bc, decay_to_end, xdt)
    chunk_decay = jnp.exp(a_cum[..., -1]).transpose(3, 0, 1, 2)

    def carry_state(h, inputs):
        st, dec = inputs
        return h * dec[..., None, None] + st, h

    h0 = jnp.zeros((b, g, r, p, n), xs.dtype)
    _, h_in = lax.scan(carry_state, h0, (chunk_states, chunk_decay))
    y_off = jnp.einsum('bclgn,cbgrpn,bgrcl->bclgrp', cc, h_in, jnp.exp(a_cum))
    return (y_diag + y_off).reshape(b, s, g, r, p)


def hybrid_layer(x, norm1_w, w_in, q_norm_w, k_norm_w, ssm_conv_w, ssm_conv_b, dt_bias, a_log,
                 d_skip, ssm_norm_w, w_out, norm2_w, w_up, ffn_conv_w, ffn_conv_b, w_down):
    b, s, _ = x.shape
    h = rms_norm(x, norm1_w)
    proj = h @ w_in
    q, k, v, z, xbc, dt_raw = jnp.split(proj, SPLIT_POINTS, axis=-1)

    q = rms_norm(q.reshape(b, s, ATTN_HEADS, ATTN_HEAD_DIM), q_norm_w).transpose(0, 2, 1, 3)
    k = rms_norm(k.reshape(b, s, ATTN_HEADS, ATTN_HEAD_DIM), k_norm_w).transpose(0, 2, 1, 3)
    v = v.reshape(b, s, ATTN_HEADS, ATTN_HEAD_DIM).transpose(0, 2, 1, 3)
    attn = moba_attention(q, k, v).transpose(0, 2, 1, 3).reshape(b, s, ATTN_WIDTH)

    xbc = jax.nn.silu(causal_dwconv(xbc, ssm_conv_w, ssm_conv_b))
    xs, bm, cm = jnp.split(xbc, (SSM_WIDTH, SSM_WIDTH + SSM_GROUPS * SSM_STATE), axis=-1)
    xs = xs.astype(jnp.float32).reshape(b, s, SSM_GROUPS, SSM_HEADS_PER_GROUP, SSM_HEAD_DIM)
    bm = bm.astype(jnp.float32).reshape(b, s, SSM_GROUPS, SSM_STATE)
    cm = cm.astype(jnp.float32).reshape(b, s, SSM_GROUPS, SSM_STATE)
    dt = jax.nn.softplus(dt_raw.astype(jnp.float32) + dt_bias.astype(jnp.float32))
    dt = dt.reshape(b, s, SSM_GROUPS, SSM_HEADS_PER_GROUP)
    a = -jnp.exp(a_log.astype(jnp.float32)).reshape(SSM_GROUPS, SSM_HEADS_PER_GROUP)
    y = ssd_chunked_scan(xs, dt, a, bm, cm)
    y = y + d_skip.astype(jnp.float32).reshape(SSM_GROUPS, SSM_HEADS_PER_GROUP)[:, :, None] * xs
    gate = jax.nn.silu(z.astype(jnp.float32)).reshape(b, s, SSM_GROUPS, SSM_HEADS_PER_GROUP * SSM_HEAD_DIM)
    y = y.reshape(b, s, SSM_GROUPS, SSM_HEADS_PER_GROUP * SSM_HEAD_DIM) * gate
    y = rms_norm(y, ssm_norm_w.reshape(SSM_GROUPS, SSM_HEADS_PER_GROUP * SSM_HEAD_DIM))
    ssm = y.reshape(b, s, SSM_WIDTH).astype(x.dtype)

    x = x + jnp.concatenate([attn, ssm], axis=-1) @ w_out

    h = rms_norm(x, norm2_w)
    u = causal_dwconv(h @ w_up, ffn_conv_w, ffn_conv_b)
    u_gate, u_val = jnp.split(u, 2, axis=-1)
    return x + (jax.nn.silu(u_gate) * u_val) @ w_down


def setup_inputs(seed: int = 0) -> dict:
    key = jax.random.key(seed)
    ks = jax.random.split(key, 17)

    def nrm(k, shape, scale):
        return jax.random.normal(k, shape, jnp.float32) * scale

    x = nrm(ks[0], (BATCH, SEQ, D_MODEL), 1.0)
    norm1_w = 1.0 + nrm(ks[1], (DEPTH, D_MODEL), 0.02)
    w_in = nrm(ks[2], (DEPTH, D_MODEL, IN_COLS), D_MODEL ** -0.5)
    q_norm_w = 1.0 + nrm(ks[3], (DEPTH, ATTN_HEAD_DIM), 0.02)
    k_norm_w = 1.0 + nrm(ks[4], (DEPTH, ATTN_HEAD_DIM), 0.02)
    ssm_conv_w = nrm(ks[5], (DEPTH, SSM_CONV, XBC_WIDTH), SSM_CONV ** -0.5)
    ssm_conv_b = nrm(ks[6], (DEPTH, XBC_WIDTH), 0.02)
    dt0 = jnp.exp(jax.random.uniform(ks[7], (DEPTH, SSM_HEADS), jnp.float32,
                                     minval=math.log(1e-3), maxval=math.log(1e-1)))
    dt_bias = dt0 + jnp.log(-jnp.expm1(-dt0))
    a_log = jnp.log(jax.random.uniform(ks[8], (DEPTH, SSM_HEADS), jnp.float32, minval=1.0, maxval=16.0))
    d_skip = 1.0 + nrm(ks[9], (DEPTH, SSM_HEADS), 0.1)
    ssm_norm_w = 1.0 + nrm(ks[10], (DEPTH, SSM_WIDTH), 0.02)
    w_out = nrm(ks[11], (DEPTH, MIX_WIDTH, D_MODEL), MIX_WIDTH ** -0.5)
    norm2_w = 1.0 + nrm(ks[12], (DEPTH, D_MODEL), 0.02)
    w_up = nrm(ks[13], (DEPTH, D_MODEL, 2 * D_FF), D_MODEL ** -0.5)
    ffn_conv_w = nrm(ks[14], (DEPTH, FFN_CONV, 2 * D_FF), FFN_CONV ** -0.5)
    ffn_conv_b = nrm(ks[15], (DEPTH, 2 * D_FF), 0.02)
    w_down = nrm(ks[16], (DEPTH, D_FF, D_MODEL), D_FF ** -0.5)
    return {"x": x, "norm1_w": norm1_w, "w_in": w_in, "q_norm_w": q_norm_w, "k_norm_w": k_norm_w,
            "ssm_conv_w": ssm_conv_w, "ssm_conv_b": ssm_conv_b, "dt_bias": dt_bias, "a_log": a_log,
            "d_skip": d_skip, "ssm_norm_w": ssm_norm_w, "w_out": w_out, "norm2_w": norm2_w,
            "w_up": w_up, "ffn_conv_w": ffn_conv_w, "ffn_conv_b": ffn_conv_b, "w_down": w_down}


def reference(x, norm1_w, w_in, q_norm_w, k_norm_w, ssm_conv_w, ssm_conv_b, dt_bias, a_log,
              d_skip, ssm_norm_w, w_out, norm2_w, w_up, ffn_conv_w, ffn_conv_b, w_down):
    for i in range(DEPTH):
        x = hybrid_layer(x, norm1_w[i], w_in[i], q_norm_w[i], k_norm_w[i], ssm_conv_w[i],
                         ssm_conv_b[i], dt_bias[i], a_log[i], d_skip[i], ssm_norm_w[i], w_out[i],
                         norm2_w[i], w_up[i], ffn_conv_w[i], ffn_conv_b[i], w_down[i])
    return x
```

```python
import contextlib
import numpy as np
import ml_dtypes
import concourse.bass as bass
import concourse.mybir as mybir
from concourse.bass_utils import run_bass_kernel_spmd

F32 = mybir.dt.float32
BF16 = mybir.dt.bfloat16
AF = mybir.ActivationFunctionType
ALU = mybir.AluOpType
AX = mybir.AxisListType
NPBF = ml_dtypes.bfloat16

D = 2048
KC = 16
T = 4096
DFF = 5632
NW = 2824
EPS = 1e-6
BIG = 30000.0
SCALE = 128.0 ** -0.5


class Buf:
    __slots__ = ("name", "w", "r")

    def __init__(self, name):
        self.name = name
        self.w = None
        self.r = []


class V:
    __slots__ = ("ap", "buf")

    def __init__(self, ap, buf):
        self.ap = ap
        self.buf = buf

    def __getitem__(self, k):
        return V(self.ap[k], self.buf)

    def r(self, pat, **kw):
        return V(self.ap.rearrange(pat, **kw), self.buf)

    def bc(self, shape):
        return V(self.ap.broadcast_to(shape), self.buf)

    def us(self, axis):
        return V(self.ap.unsqueeze(axis), self.buf)


class Tile:
    def __init__(self, t, name):
        self.t = t
        self.buf = Buf(name)

    def __getitem__(self, k):
        return V(self.t[k], self.buf)


class DT:
    def __init__(self, ap, name):
        self.ap = ap
        self.name = name
        self.bufs = {}

    def v(self, key, ap):
        b = self.bufs.get(key)
        if b is None:
            b = Buf(f"{self.name}{key}")
            self.bufs[key] = b
        return V(ap, b)


class Op:
    __slots__ = ("eng", "fn", "kind", "waits", "signal", "eidx", "sigval", "dsem", "dval")


COMPUTE = ("pe", "act", "dve", "pool")
DMAQ = ("sp", "pool")
KD = 6


class Sched:
    def __init__(self, nc):
        self.nc = nc
        self.ops = {e: [] for e in ("pe", "act", "dve", "pool", "sp")}
        self.seen = {e: {} for e in self.ops}
        self.ndma = {q: 0 for q in DMAQ}
        self.dma_ops = {q: [] for q in DMAQ}

    def add(self, eng, fn, reads=(), writes=(), dma=False):
        op = Op()
        op.eng = eng
        op.fn = fn
        op.kind = "d" if dma else "c"
        op.signal = False
        op.waits = []
        op.eidx = len(self.ops[eng])
        op.sigval = None
        reads = [x.buf if isinstance(x, (V, Tile)) else x for x in reads]
        writes = [x.buf if isinstance(x, (V, Tile)) else x for x in writes]
        deps = []
        for b in reads:
            if b.w is not None:
                deps.append((b.w, True))
        for b in writes:
            if b.w is not None:
                deps.append((b.w, False))
            for r in b.r:
                deps.append((r, False))
        if dma:
            q = eng
            j = self.ndma[q]
            self.ndma[q] += 1
            op.dsem = (q, j % KD)
            op.dval = 16 * (j // KD + 1)
            if j >= KD:
                deps.append((self.dma_ops[q][j - KD], True))
            self.dma_ops[q].append(op)
        seen = self.seen[eng]
        for p, raw in deps:
            if p is op:
                continue
            if p.kind == "c":
                if p.eng == eng and (eng == "pe" or not raw):
                    continue
                key = ("c", p.eng)
                if seen.get(key, -1) >= p.eidx:
                    continue
                seen[key] = p.eidx
                p.signal = True
                op.waits.append(p)
            else:
                key = ("d",) + p.dsem
                if seen.get(key, -1) >= p.dval:
                    continue
                seen[key] = p.dval
                op.waits.append(p)
        self.ops[eng].append(op)
        for b in reads:
            b.r.append(op)
        for b in writes:
            b.w = op
            b.r = []
        return op

    def dma(self, q, out, in_, er=(), ew=()):
        o, i = out.ap, in_.ap
        return self.add(q, lambda e: e.dma_start(out=o, in_=i), [in_, *er], [out, *ew], dma=True)

    def mm(self, out, lhsT, rhs, start, stop, er=()):
        o, l, r = out.ap, lhsT.ap, rhs.ap
        return self.add("pe", lambda e: e.matmul(o, l, r, start=start, stop=stop), [lhsT, rhs, *er], [out])

    def tr(self, out, in_, ident):
        o, i, d = out.ap, in_.ap, ident.ap
        return self.add("pe", lambda e: e.transpose(o, i, d), [in_, ident], [out])

    def act(self, out, in_, func, bias=None, scale=None, eng="act"):
        o, i = out.ap, in_.ap
        kw = {}
        rd = [in_]
        if bias is not None:
            if isinstance(bias, V):
                kw["bias"] = bias.ap
                rd.append(bias)
            else:
                kw["bias"] = bias
        if scale is not None:
            if isinstance(scale, V):
                kw["scale"] = scale.ap
                rd.append(scale)
            else:
                kw["scale"] = scale
        return self.add(eng, lambda e: e.activation(o, i, func, **kw), rd, [out])

    def tt(self, out, in0, in1, op, eng="dve"):
        o, a, b = out.ap, in0.ap, in1.ap
        return self.add(eng, lambda e: e.tensor_tensor(o, a, b, op), [in0, in1], [out])

    def ts(self, out, in0, s1, s2, op0, op1=None, eng="dve"):
        o, a = out.ap, in0.ap
        rd = [in0]
        if isinstance(s1, V):
            rd.append(s1)
            s1 = s1.ap
        if isinstance(s2, V):
            rd.append(s2)
            s2 = s2.ap
        if op1 is None:
            return self.add(eng, lambda e: e.tensor_scalar(o, a, s1, None, op0), rd, [out])
        return self.add(eng, lambda e: e.tensor_scalar(o, a, s1, s2, op0, op1), rd, [out])

    def stt(self, out, in0, sc, in1, op0, op1, eng="dve"):
        o, a, b = out.ap, in0.ap, in1.ap
        rd = [in0, in1]
        if isinstance(sc, V):
            rd.append(sc)
            sc = sc.ap
        return self.add(eng, lambda e: e.scalar_tensor_tensor(o, a, sc, b, op0, op1), rd, [out])

    def cp(self, out, in_, eng="dve"):
        o, i = out.ap, in_.ap
        return self.add(eng, lambda e: e.tensor_copy(o, i), [in_], [out])

    def recip(self, out, in_):
        o, i = out.ap, in_.ap
        return self.add("dve", lambda e: e.reciprocal(o, i), [in_], [out])

    def red(self, out, in_, op=ALU.add):
        o, i = out.ap, in_.ap
        return self.add("dve", lambda e: e.tensor_reduce(o, i, AX.X, op), [in_], [out])

    def max8(self, out, in_):
        o, i = out.ap, in_.ap
        return self.add("dve", lambda e: e.max(o, i), [in_], [out])

    def memset(self, out, val, eng="dve"):
        o = out.ap
        return self.add(eng, lambda e: e.memset(o, val), [], [out])

    def emit(self):
        nc = self.nc
        for e in COMPUTE:
            c = 0
            for op in self.ops[e]:
                if op.kind == "c" and op.signal:
                    c += 1
                    op.sigval = c
        with contextlib.ExitStack() as st:
            csem = {e: st.enter_context(nc.semaphore("c_" + e)) for e in COMPUTE}
            dsem = {(q, i): st.enter_context(nc.semaphore(f"d_{q}{i}")) for q in DMAQ for i in range(KD)}
            block = st.enter_context(nc.Block())

            def run(name, e):
                for op in self.ops[name]:
                    for p in op.waits:
                        if p.kind == "c":
                            e.wait_ge(csem[p.eng], p.sigval)
                        else:
                            e.wait_ge(dsem[p.dsem], p.dval)
                    inst = op.fn(e)
                    if op.kind == "d":
                        inst.then_inc(dsem[op.dsem], 16)
                    elif op.signal:
                        inst.then_inc(csem[name], 1)
                if name in DMAQ:
                    n = self.ndma[name]
                    for i in range(min(KD, n)):
                        e.wait_ge(dsem[(name, i)], 16 * ((n - 1 - i) // KD + 1))

            @block.sync
            def _(e):
                run("sp", e)

            @block.tensor
            def _(e):
                run("pe", e)

            @block.scalar
            def _(e):
                run("act", e)

            @block.vector
            def _(e):
                run("dve", e)

            @block.gpsimd
            def _(e):
                run("pool", e)


class Ctx:
    def __init__(self):
        self.nc = bass.Bass("TRN2", target_bir_lowering=False)
        self.S = Sched(self.nc)
        self.n = 0

    def sb(self, shape, dt, name=None):
        self.n += 1
        name = name or f"sb{self.n}"
        return Tile(self.nc.alloc_sbuf_tensor(name, list(shape), dt), name)

    def ps(self, shape, dt=F32, name=None):
        self.n += 1
        name = name or f"ps{self.n}"
        return Tile(self.nc.alloc_psum_tensor(name, list(shape), dt), name)

    def din(self, name, shape, dt):
        return DT(self.nc.dram_tensor(name, list(shape), dt, kind="ExternalInput").ap(), name)

    def dout(self, name, shape, dt):
        return DT(self.nc.dram_tensor(name, list(shape), dt, kind="ExternalOutput").ap(), name)

    debug = False

    def dscr(self, name, shape, dt):
        if self.debug:
            return DT(self.nc.dram_tensor(name, list(shape), dt, kind="ExternalOutput").ap(), name)
        return DT(self.nc.dram_tensor(name, list(shape), dt).ap(), name)


class Rot:
    def __init__(self, tiles):
        self.tiles = tiles
        self.i = 0

    def get(self):
        t = self.tiles[self.i % len(self.tiles)]
        self.i += 1
        return t


def const_tables():
    c = {}
    c["ident32"] = np.eye(128, dtype=np.float32)
    c["identb"] = np.eye(128, dtype=np.float32).astype(NPBF)
    c["c128"] = np.full((128, 128), 1.0 / 128, np.float32).astype(NPBF)
    c["c2048"] = np.full((128, 128), 1.0 / 2048, np.float32).astype(NPBF)
    c["onesb"] = np.ones((128, 128), np.float32).astype(NPBF)
    c["ones32"] = np.ones((128, 128), np.float32)
    t = np.arange(128)
    c["tri32"] = (t[:, None] <= t[None, :]).astype(np.float32)
    c["segmask"] = np.where(t[:, None] <= t[None, :], 0.0, -BIG).astype(np.float32)
    neg1 = np.zeros((8, 4, 16), np.float32)
    wtab = np.zeros((8, 4, 16), np.float32)
    for qt in range(8):
        for s in range(4):
            own = 2 * qt + s // 2
            for j in range(16):
                neg1[qt, s, j] = -BIG if j >= own else 0.0
                if j == own:
                    wtab[qt, s, j] = 0.0
                elif j < own:
                    wtab[qt, s, j] = -BIG
                else:
                    wtab[qt, s, j] = -2 * BIG
    c["neg1"] = np.broadcast_to(neg1.reshape(1, 512), (128, 512)).copy()
    c["wtab"] = np.broadcast_to(wtab.reshape(1, 512), (128, 512)).copy()
    trim = np.zeros((128, 4, 512), np.float32)
    for m in range(4):
        for s in range(4):
            if m // 2 != s // 2:
                continue
            blk = trim[:, m, s * 128:(s + 1) * 128]
            if s < m:
                blk[:] = -BIG
            elif s == m:
                blk[:] = np.where(t[:, None] <= t[None, :], 0.0, -BIG)
    c["trim"] = trim.astype(NPBF)
    esel = np.zeros((16, 16, 128), np.float32)
    for j in range(16):
        esel[j, j, :] = 1.0
    c["esel"] = esel.reshape(16, 2048).astype(NPBF)
    selr = np.zeros((8, 8, 128), np.float32)
    for r in range(8):
        selr[r, r, :] = 1.0
    c["selr"] = selr.reshape(8, 1024)
    return c


CONST_SPECS = [("ident32", [128, 128], F32), ("identb", [128, 128], BF16), ("c128", [128, 128], BF16),
               ("c2048", [128, 128], BF16), ("onesb", [128, 128], BF16), ("ones32", [128, 128], F32),
               ("tri32", [128, 128], F32), ("segmask", [128, 128], F32), ("neg1", [128, 512], F32),
               ("wtab", [128, 512], F32), ("trim", [128, 4, 512], BF16), ("esel", [16, 2048], BF16),
               ("selr", [8, 1024], F32)]


def load_consts(cx, names):
    S = cx.S
    out = {}
    for nm, shp, dt in CONST_SPECS:
        if nm not in names:
            continue
        d = cx.din(nm, shp, dt)
        t = cx.sb(shp, dt, name="k_" + nm)
        S.dma("sp", t[:], d.v(0, d.ap))
        out[nm] = t
    return out


def cast_weight(cx, src, dst, rows, blk=128):
    S = cx.S
    for i in range(rows // blk):
        S.dma("pool", dst.v(i, dst.ap[i * blk:(i + 1) * blk, :]), src.v(0, src.ap[i * blk:(i + 1) * blk, :]))


def wview(dst, c0, n, nk):
    ap = dst.ap.rearrange("(k p) c -> p k c", p=128)[:, :, c0:c0 + n]
    return ap


def build_pab(stage=3, debug=False):
    cx = Ctx()
    cx.debug = debug
    S = cx.S
    nc = cx.nc
    xT = cx.din("xT", [D, T], F32)
    n1w = cx.din("n1w", [128, KC], F32)
    w_in = cx.din("w_in", [D, NW], F32)
    qkw = cx.din("qkw", [128, 2], F32)
    cw = cx.din("cw", [128, 6, 4], F32)
    cb = cx.din("cb", [128, 6], F32)
    hv = cx.din("hv", [3, 8], F32)
    snw = cx.din("snw", [1, 512], F32)
    mixT = cx.dout("mixT", [1024, T], BF16)
    K = load_consts(cx, {n for n, _, _ in CONST_SPECS})

    win_s = cx.dscr("win_s", [D, NW], BF16)
    qk_s = cx.dscr("qk_s", [1024, T], BF16)
    v_s = cx.dscr("v_s", [T, 512], BF16)
    zs_s = cx.dscr("zs_s", [T, 512], F32)
    xs_s = cx.dscr("xs_s", [T, 512], F32)
    b_s = cx.dscr("b_s", [T, 128], BF16)
    bt_s = cx.dscr("bt_s", [128, T], BF16)
    ct_s = cx.dscr("ct_s", [128, T], BF16)

    cast_weight(cx, w_in, win_s, D)
    WIN_KEYS = list(range(KC))

    def wload(tile, c0, n):
        ap = wview(win_s, c0, n, KC)
        er = [win_s.v(i, win_s.ap) for i in WIN_KEYS]
        S.dma("sp", tile[:, :, 0:n], V(ap, er[0].buf), er=er[1:])

    n1t = cx.sb([128, KC], F32)
    S.dma("sp", n1t[:], n1w.v(0, n1w.ap))
    qkt = cx.sb([128, 2], F32)
    S.dma("sp", qkt[:], qkw.v(0, qkw.ap))
    cwt = cx.sb([128, 6, 4], F32)
    S.dma("sp", cwt[:], cw.v(0, cw.ap))
    cbt = cx.sb([128, 6], F32)
    S.dma("sp", cbt[:], cb.v(0, cb.ap))
    hvt = cx.sb([128, 3, 8], F32)
    for i in range(3):
        S.dma("sp", hvt[:, i, :], hv.v(0, hv.ap[i:i + 1, :].partition_broadcast(128)))
    snt = cx.sb([128, 512], F32)
    S.dma("sp", snt[:], snw.v(0, snw.ap.partition_broadcast(128)))
    a_b = cx.sb([128, 8], F32)
    S.act(a_b[:], hvt[:, 1, :], AF.Exp)
    S.ts(a_b[:], a_b[:], -1.0, None, ALU.mult)

    dt_all = cx.sb([128, 32, 8], F32)
    la_all = cx.sb([128, 32, 8], F32)
    kms = cx.sb([128, 4, 16], F32)
    kmb = cx.sb([128, 4, 16], BF16)
    carr = cx.sb([128, 6, 3], F32)
    S.memset(carr[:], 0.0)

    xq = Rot([cx.sb([128, 512], F32) for _ in range(3)])
    sqq = Rot([cx.sb([128, 512], BF16) for _ in range(2)])
    hn = cx.sb([128, KC, 512], BF16)
    wq = Rot([cx.sb([128, KC, 128], BF16) for _ in range(3)])
    wbig = Rot([cx.sb([128, KC, 512], BF16) for _ in range(2)])
    f32q = Rot([cx.sb([128, 512], F32) for _ in range(6)])
    b16q = Rot([cx.sb([128, 512], BF16) for _ in range(4)])
    ubuf = Rot([cx.sb([128, 515], F32) for _ in range(2)])
    rsn = cx.sb([128, 512], F32)
    pA = Rot([cx.ps([128, 512]) for _ in range(3)])
    pB = Rot([cx.ps([128, 512]) for _ in range(2)])
    pO = cx.ps([128, 512])
    pDen = cx.ps([128, 512])
    pG = cx.ps([128, 512])
    pss = pG

    for j in range(8):
        t0 = j * 512
        for k in range(KC):
            xc = xq.get()
            S.dma("sp", xc[:], xT.v(0, xT.ap[k * 128:(k + 1) * 128, t0:t0 + 512]))
            sq = sqq.get()
            S.act(sq[:], xc[:], AF.Square)
            S.mm(pss[:], K["c2048"][:], sq[:], k == 0, k == KC - 1)
        S.act(rsn[:], pss[:], AF.Sqrt, bias=EPS, scale=1.0)
        S.recip(rsn[:], rsn[:])
        for k in range(KC):
            xc = xq.get()
            S.dma("sp", xc[:], xT.v(0, xT.ap[k * 128:(k + 1) * 128, t0:t0 + 512]))
            S.stt(hn[:, k, :], xc[:], n1t[:, k:k + 1], rsn[:], ALU.mult, ALU.mult)
        for cc in range(8):
            w = wq.get()
            wload(w, cc * 128, 128)
            pa = pA.get()
            for k in range(KC):
                S.mm(pa[:], w[:, k, :], hn[:, k, :], k == 0, k == KC - 1)
            sq = sqq.get()
            S.act(sq[:], pa[:], AF.Square)
            pb = pB.get()
            S.mm(pb[:], K["c128"][:], sq[:], True, True)
            rs = f32q.get()
            S.act(rs[:], pb[:], AF.Sqrt, bias=EPS, scale=1.0)
            S.recip(rs[:], rs[:])
            qn = b16q.get()
            wi = 0 if cc < 4 else 1
            S.stt(qn[:], pa[:], qkt[:, wi:wi + 1], rs[:], ALU.mult, ALU.mult)
            S.dma("sp", qk_s.v((cc, j), qk_s.ap[cc * 128:(cc + 1) * 128, t0:t0 + 512]), qn[:])
            if cc >= 4:
                S.red(kms[:, cc - 4, 2 * j:2 * j + 2], qn[:].r("p (b t) -> p b t", b=2))
        for which in range(2):
            w = wbig.get()
            wload(w, 1024 + which * 512, 512)
            for s in range(4):
                pa = pA.get()
                for k in range(KC):
                    S.mm(pa[:], hn[:, k, s * 128:(s + 1) * 128], w[:, k, :], k == 0, k == KC - 1)
                r0 = t0 + s * 128
                if which == 0:
                    o = b16q.get()
                    S.act(o[:], pa[:], AF.Copy)
                    S.dma("sp", v_s.v((j, s), v_s.ap[r0:r0 + 128, :]), o[:])
                else:
                    o = f32q.get()
                    S.act(o[:], pa[:], AF.Silu)
                    S.dma("sp", zs_s.v((j, s), zs_s.ap[r0:r0 + 128, :]), o[:])
        for c in range(6):
            w = wq.get()
            wload(w, 2048 + c * 128, 128)
            pa = pA.get()
            for k in range(KC):
                S.mm(pa[:], w[:, k, :], hn[:, k, :], k == 0, k == KC - 1)
            ub = ubuf.get()
            S.cp(ub[:, 0:3], carr[:, c, :])
            S.act(ub[:, 3:515], pa[:], AF.Copy)
            S.cp(carr[:, c, :], ub[:, 512:515])
            acc = f32q.get()
            S.ts(acc[:], ub[:, 3:515], cwt[:, c, 3:4], cbt[:, c:c + 1], ALU.mult, ALU.add)
            for tap in (2, 1, 0):
                S.stt(acc[:], ub[:, tap:tap + 512], cwt[:, c, tap:tap + 1], acc[:], ALU.mult, ALU.add)
            if c < 4:
                xf = f32q.get()
                S.act(xf[:], acc[:], AF.Silu)
                pt = pB.get()
                for s in range(4):
                    S.tr(pt[:, s * 128:(s + 1) * 128], xf[:, s * 128:(s + 1) * 128], K["ident32"][:])
                xo = f32q.get()
                S.cp(xo[:], pt[:])
                S.dma("sp", xs_s.v((j, c), xs_s.ap[t0:t0 + 512, c * 128:(c + 1) * 128].rearrange("(s p) f -> p s f", p=128)),
                      xo[:].r("p (s f) -> p s f", s=4))
            elif c == 4:
                xf = f32q.get()
                S.act(xf[:], acc[:], AF.Silu)
                bb = b16q.get()
                S.cp(bb[:], xf[:])
                S.dma("sp", bt_s.v(j, bt_s.ap[:, t0:t0 + 512]), bb[:])
                pt = pB.get()
                for s in range(4):
                    S.tr(pt[:, s * 128:(s + 1) * 128], xf[:, s * 128:(s + 1) * 128], K["ident32"][:])
                bo = b16q.get()
                S.cp(bo[:], pt[:])
                S.dma("sp", b_s.v(j, b_s.ap[t0:t0 + 512, :].rearrange("(s p) f -> p s f", p=128)),
                      bo[:].r("p (s f) -> p s f", s=4))
            else:
                cbf = b16q.get()
                S.act(cbf[:], acc[:], AF.Silu)
                S.dma("sp", ct_s.v(j, ct_s.ap[:, t0:t0 + 512]), cbf[:])
        w = wq.get()
        wload(w, 2816, 8)
        for s in range(4):
            pb = pB.get()
            for k in range(KC):
                S.mm(pb[:, 0:8], hn[:, k, s * 128:(s + 1) * 128], w[:, k, 0:8], k == 0, k == KC - 1)
            ci = j * 4 + s
            S.tt(dt_all[:, ci, :], pb[:, 0:8], hvt[:, 0, :], ALU.add)
            S.act(dt_all[:, ci, :], dt_all[:, ci, :], AF.Exp)
            S.act(dt_all[:, ci, :], dt_all[:, ci, :], AF.Ln, bias=1.0, scale=1.0)
            S.tt(la_all[:, ci, :], dt_all[:, ci, :], a_b[:], ALU.mult)
    S.cp(kmb[:], kms[:])
    if stage < 2:
        S.emit()
        return nc

    kT = cx.sb([128, T], BF16)
    vh = cx.sb([128, 32, 128], BF16)
    qtq = Rot([cx.sb([128, 512], BF16) for _ in range(2)])
    ptq = Rot([cx.sb([128, 512], BF16) for _ in range(3)])
    gm = cx.sb([128, 4, 16], F32)
    m8 = cx.sb([128, 4, 8], F32)
    tsel = cx.sb([128, 4, 16], F32)
    bq = cx.sb([128, 4, 16], F32)
    biasT = cx.sb([16, 512], BF16)
    qk_keys = {}
    for h in range(4):
        kbufs = [qk_s.v((4 + h, j), qk_s.ap) for j in range(8)]
        S.dma("sp", kT[:], V(qk_s.ap[512 + h * 128:512 + (h + 1) * 128, :], kbufs[0].buf), er=kbufs[1:])
        vbufs = [v_s.v((j, s), v_s.ap) for j in range(8) for s in range(4)]
        S.dma("sp", vh[:], V(v_s.ap[:, h * 128:(h + 1) * 128].rearrange("(k p) d -> p k d", p=128), vbufs[0].buf),
              er=vbufs[1:])
        for qt in range(8):
            q0 = qt * 512
            qT = qtq.get()
            S.dma("sp", qT[:], qk_s.v((h, qt), qk_s.ap[h * 128:(h + 1) * 128, q0:q0 + 512]))
            for s in range(4):
                S.mm(pG[:, s * 16:(s + 1) * 16], qT[:, s * 128:(s + 1) * 128], kmb[:, h, :], True, True)
            S.tt(gm[:].r("p s j -> p (s j)"), pG[:, 0:64], K["neg1"][:, qt * 64:(qt + 1) * 64], ALU.add)
            for s in range(4):
                S.max8(m8[:, s, :], gm[:, s, :])
            for s in range(4):
                S.ts(tsel[:, s, :], gm[:, s, :], m8[:, s, 2:3], None, ALU.is_ge)
            S.stt(bq[:].r("p s j -> p (s j)"), tsel[:].r("p s j -> p (s j)"), BIG,
                  K["wtab"][:, qt * 64:(qt + 1) * 64], ALU.mult, ALU.add)
            S.ts(bq[:], bq[:], 0.0, None, ALU.min)
            for s in range(4):
                S.tr(pG[0:16, s * 128:(s + 1) * 128], bq[:, s, :], K["ident32"][:])
            S.act(biasT[:], pG[0:16, :], AF.Copy)
            nkt = 4 * qt + 4
            for kt in range(nkt):
                jb = kt // 2
                pS = pA.get()
                diag = kt >= 4 * qt
                S.mm(pS[:], kT[:, kt * 128:(kt + 1) * 128], qT[:], True, False)
                S.mm(pS[:], K["esel"][:, jb * 128:(jb + 1) * 128], biasT[:], False, not diag)
                if diag:
                    S.mm(pS[:], K["identb"][:], K["trim"][:, kt - 4 * qt, :], False, True)
                pT = ptq.get()
                S.act(pT[:], pS[:], AF.Exp, scale=SCALE)
                S.mm(pO[:], vh[:, kt, :], pT[:], kt == 0, kt == nkt - 1)
                S.mm(pDen[:], K["onesb"][:], pT[:], kt == 0, kt == nkt - 1)
            rc = f32q.get()
            S.recip(rc[:], pDen[:])
            ao = b16q.get()
            S.tt(ao[:], pO[:], rc[:], ALU.mult)
            S.dma("sp", mixT.v((h, qt), mixT.ap[h * 128:(h + 1) * 128, q0:q0 + 512]), ao[:])

    if stage < 3:
        S.emit()
        return nc
    H = cx.sb([128, 512], F32)
    Hb = cx.sb([128, 512], BF16)
    S.memset(H[:], 0.0)
    S.memset(Hb[:], 0.0)
    xsq = Rot([cx.sb([128, 512], F32) for _ in range(2)])
    zsq = Rot([cx.sb([128, 512], F32) for _ in range(2)])
    btq = Rot([cx.sb([128, 128], BF16) for _ in range(2)])
    ctq = Rot([cx.sb([128, 128], BF16) for _ in range(2)])
    bmq = Rot([cx.sb([128, 128], BF16) for _ in range(2)])
    acum = cx.sb([128, 8], F32)
    nacum = cx.sb([128, 8], F32)
    acT = cx.sb([8, 128], F32)
    cdec = cx.sb([128, 8], F32)
    dte = cx.sb([128, 8], F32)
    eac = cx.sb([128, 8], F32)
    Er = Rot([cx.sb([128, 128], F32) for _ in range(3)])
    Mall = cx.sb([128, 8, 128], BF16)
    xdtf = cx.sb([128, 512], F32)
    xdtb = cx.sb([128, 512], BF16)
    xdt2 = cx.sb([128, 512], BF16)
    ssum = cx.sb([128, 1], F32)
    ymq = Rot([cx.sb([128, 512], BF16) for _ in range(2)])
    dskb = hvt[:, 2, :]

    def b3(v):
        return v.us(2).bc([128, 8, 64])

    def r3(v):
        return v.r("p (r j) -> p r j", r=8)

    for c in range(32):
        j, s = c // 4, c % 4
        r0 = c * 128
        xs = xsq.get()
        S.dma("sp", xs[:], V(xs_s.ap[r0:r0 + 128, :], xs_s.v((j, 0), xs_s.ap).buf),
              er=[xs_s.v((j, i), xs_s.ap) for i in range(1, 4)])
        zs = zsq.get()
        S.dma("sp", zs[:], zs_s.v((j, s), zs_s.ap[r0:r0 + 128, :]))
        bt = btq.get()
        S.dma("sp", bt[:], bt_s.v(j, bt_s.ap[:, r0:r0 + 128]))
        ct = ctq.get()
        S.dma("sp", ct[:], ct_s.v(j, ct_s.ap[:, r0:r0 + 128]))
        bm = bmq.get()
        S.dma("sp", bm[:], b_s.v(j, b_s.ap[r0:r0 + 128, :]))
        la = la_all[:, c, :]
        dtc = dt_all[:, c, :]
        p1 = pB.get()
        S.mm(p1[:, 0:8], K["tri32"][:], la, True, True)
        S.mm(p1[:, 8:16], K["ones32"][:], la, True, True)
        S.cp(acum[:], p1[:, 0:8])
        S.ts(nacum[:], p1[:, 0:8], -1.0, None, ALU.mult)
        S.act(cdec[:], p1[:, 8:16], AF.Exp)
        S.tt(dte[:], p1[:, 8:16], acum[:], ALU.subtract)
        S.act(dte[:], dte[:], AF.Exp)
        S.act(eac[:], acum[:], AF.Exp)
        S.tr(p1[0:8, 128:256], acum[:], K["ident32"][:])
        S.cp(acT[:], p1[0:8, 128:256])
        S.tt(r3(xdtf[:]), r3(xs[:]), b3(dtc), ALU.mult)
        S.act(xdtb[:], xdtf[:], AF.Copy)
        S.tt(r3(xdt2[:]), r3(xdtf[:]), b3(dte[:]), ALU.mult)
        pcb = pB.get()
        S.mm(pcb[:, 0:128], bt[:], ct[:], True, True)
        for r in range(8):
            pseg = pA.get()
            S.mm(pseg[:, 0:128], K["selr"][:, r * 128:(r + 1) * 128], acT[:], True, False)
            S.mm(pseg[:, 0:128], K["ident32"][:], K["segmask"][:], False, True)
            er_ = Er.get()
            S.act(er_[:], pseg[:, 0:128], AF.Exp, bias=nacum[:, r:r + 1], scale=1.0)
            S.tt(Mall[:, r, :], pcb[:, 0:128], er_[:], ALU.mult)
        py = pA.get()
        for r in range(8):
            S.mm(py[:, r * 64:(r + 1) * 64], Mall[:, r, :], xdtb[:, r * 64:(r + 1) * 64], True, True)
        pyo = pA.get()
        S.mm(pyo[:], ct[:], Hb[:], True, True)
        y = f32q.get()
        S.tt(r3(y[:]), r3(pyo[:]), b3(eac[:]), ALU.mult)
        S.tt(y[:], y[:], py[:], ALU.add)
        t2 = f32q.get()
        S.tt(r3(t2[:]), r3(xs[:]), b3(dskb), ALU.mult)
        S.tt(y[:], y[:], t2[:], ALU.add)
        pst = pA.get()
        S.mm(pst[:], bm[:], xdt2[:], True, True)
        S.tt(r3(H[:]), r3(H[:]), b3(cdec[:]), ALU.mult)
        S.tt(H[:], H[:], pst[:], ALU.add)
        S.act(Hb[:], H[:], AF.Copy)
        S.tt(y[:], y[:], zs[:], ALU.mult)
        S.tt(t2[:], y[:], y[:], ALU.mult)
        S.red(ssum[:], t2[:])
        S.act(ssum[:], ssum[:], AF.Sqrt, bias=EPS, scale=1.0 / 512)
        S.recip(ssum[:], ssum[:])
        S.stt(y[:], y[:], ssum[:, 0:1], snt[:], ALU.mult, ALU.mult)
        pyt = pB.get()
        for f in range(4):
            S.tr(pyt[:, f * 128:(f + 1) * 128], y[:, f * 128:(f + 1) * 128], K["ident32"][:])
        ym = ymq.get()
        S.act(ym[:], pyt[:], AF.Copy)
        S.dma("sp", mixT.v(("s", c), mixT.ap[512:1024, r0:r0 + 128].rearrange("(f p) t -> p f t", p=128)),
              ym[:].r("p (f t) -> p f t", f=4))
    print("PAB sbuf remaining", nc.sbuf_bytes_remaining)
    S.emit()
    return nc


def build_pc():
    cx = Ctx()
    S = cx.S
    nc = cx.nc
    NT = 2050
    xT = cx.din("xT", [D, NT], F32)
    mT = cx.din("mT", [D, NT], BF16)
    n2w = cx.din("n2w", [128, KC], F32)
    w_out = cx.din("w_out", [D, D], F32)
    w_up = cx.din("w_up", [D, 2 * DFF], F32)
    w_dn = cx.din("w_dn", [DFF, D], F32)
    fcw = cx.din("fcw", [128, 88, 3], F32)
    fcb = cx.din("fcb", [128, 88], F32)
    oT = cx.dout("oT", [D, 2048], F32)
    K = load_consts(cx, {"c2048"})
    wout_s = cx.dscr("wout_s", [D, D], BF16)
    wup_s = cx.dscr("wup_s", [D, 2 * DFF], BF16)
    wdn_s = cx.dscr("wdn_s", [DFF, D], BF16)
    x1_s = cx.dscr("x1_s", [D, NT], F32)
    cast_weight(cx, w_out, wout_s, D)
    cast_weight(cx, w_up, wup_s, D)
    cast_weight(cx, w_dn, wdn_s, DFF)

    def wload(tile, scr, c0, n, nk):
        ap = scr.ap.rearrange("(k p) c -> p k c", p=128)[:, :, c0:c0 + n]
        er = [scr.v(i, scr.ap) for i in range(nk)]
        S.dma("sp", tile[:, 0:nk, 0:n], V(ap, er[0].buf), er=er[1:])

    n2t = cx.sb([128, KC], F32)
    S.dma("sp", n2t[:], n2w.v(0, n2w.ap))
    fcwt = cx.sb([128, 88, 3], F32)
    S.dma("sp", fcwt[:], fcw.v(0, fcw.ap))
    fcbt = cx.sb([128, 88], F32)
    S.dma("sp", fcbt[:], fcb.v(0, fcb.ap))
    ucar = cx.sb([128, 88, 2], F32)

    mixt = cx.sb([128, KC, 512], BF16)
    hn2 = cx.sb([128, KC, 512], BF16)
    gT = cx.sb([128, 44, 512], BF16)
    xq = Rot([cx.sb([128, 512], F32) for _ in range(4)])
    sqq = Rot([cx.sb([128, 512], BF16) for _ in range(2)])
    wq = Rot([cx.sb([128, KC, 128], BF16) for _ in range(4)])
    wdq = Rot([cx.sb([128, 44, 128], BF16) for _ in range(2)])
    ubq = Rot([cx.sb([128, 514], F32) for _ in range(4)])
    accq = Rot([cx.sb([128, 512], F32) for _ in range(6)])
    rsn = cx.sb([128, 512], F32)
    pA = Rot([cx.ps([128, 512]) for _ in range(6)])
    pss = cx.ps([128, 512])

    tiles = [(0, 2)] + [(2 + 512 * i, 512) for i in range(4)]
    for ti, (c0, W) in enumerate(tiles):
        halo = ti == 0
        S.dma("sp", mixt[:, :, 0:W], mT.v(0, mT.ap[:, c0:c0 + W].rearrange("(k p) t -> p k t", p=128)))
        for cc in range(KC):
            w = wq.get()
            wload(w, wout_s, cc * 128, 128, KC)
            pa = pA.get()
            for k in range(KC):
                S.mm(pa[:, 0:W], w[:, k, :], mixt[:, k, 0:W], k == 0, k == KC - 1)
            xc = xq.get()
            S.dma("sp", xc[:, 0:W], xT.v(0, xT.ap[cc * 128:(cc + 1) * 128, c0:c0 + W]))
            S.tt(xc[:, 0:W], pa[:, 0:W], xc[:, 0:W], ALU.add)
            S.dma("sp", x1_s.v((cc, ti), x1_s.ap[cc * 128:(cc + 1) * 128, c0:c0 + W]), xc[:, 0:W])
            sq = sqq.get()
            S.act(sq[:, 0:W], xc[:, 0:W], AF.Square)
            S.mm(pss[:, 0:W], K["c2048"][:], sq[:, 0:W], cc == 0, cc == KC - 1)
        S.act(rsn[:, 0:W], pss[:, 0:W], AF.Sqrt, bias=EPS, scale=1.0)
        S.recip(rsn[:, 0:W], rsn[:, 0:W])
        for cc in range(KC):
            xc = xq.get()
            S.dma("sp", xc[:, 0:W], x1_s.v((cc, ti), x1_s.ap[cc * 128:(cc + 1) * 128, c0:c0 + W]))
            S.stt(hn2[:, cc, 0:W], xc[:, 0:W], n2t[:, cc:cc + 1], rsn[:, 0:W], ALU.mult, ALU.mult)
        for c in range(44):
            res = []
            for half in range(2):
                ch = c + 44 * half
                w = wq.get()
                wload(w, wup_s, ch * 128, 128, KC)
                pa = pA.get()
                for k in range(KC):
                    S.mm(pa[:, 0:W], w[:, k, :], hn2[:, k, 0:W], k == 0, k == KC - 1)
                if halo:
                    S.cp(ucar[:, ch, :], pa[:, 0:2])
                    continue
                ub = ubq.get()
                S.cp(ub[:, 0:2], ucar[:, ch, :])
                S.act(ub[:, 2:514], pa[:], AF.Copy)
                S.cp(ucar[:, ch, :], ub[:, 512:514])
                acc = accq.get()
                eng = "dve"
                S.ts(acc[:], ub[:, 2:514], fcwt[:, ch, 2:3], fcbt[:, ch:ch + 1], ALU.mult, ALU.add, eng=eng)
                S.stt(acc[:], ub[:, 1:513], fcwt[:, ch, 1:2], acc[:], ALU.mult, ALU.add, eng=eng)
                S.stt(acc[:], ub[:, 0:512], fcwt[:, ch, 0:1], acc[:], ALU.mult, ALU.add, eng=eng)
                res.append(acc)
            if halo:
                continue
            sg = accq.get()
            S.act(sg[:], res[0][:], AF.Silu)
            S.tt(gT[:, c, :], sg[:], res[1][:], ALU.mult)
        if halo:
            continue
        o0 = c0 - 2
        for cc in range(KC):
            wd = wdq.get()
            wload(wd, wdn_s, cc * 128, 128, 44)
            pa = pA.get()
            for f in range(44):
                S.mm(pa[:], wd[:, f, :], gT[:, f, :], f == 0, f == 43)
            xc = xq.get()
            S.dma("sp", xc[:], x1_s.v((cc, ti), x1_s.ap[cc * 128:(cc + 1) * 128, c0:c0 + 512]))
            S.tt(xc[:], pa[:], xc[:], ALU.add)
            S.dma("sp", oT.v((cc, ti), oT.ap[cc * 128:(cc + 1) * 128, o0:o0 + 512]), xc[:])
    print("PC sbuf remaining", nc.sbuf_bytes_remaining)
    S.emit()
    return nc


def role_cols(r):
    cols = []
    for base in (0, 1024, 2048, 3072, 4096):
        cols.append(np.arange(base + 512 * r, base + 512 * r + 512))
    cols.append(np.arange(4096 + 1024 + 128 * r, 4096 + 1024 + 128 * r + 128))
    cols.append(np.arange(4096 + 1280 + 128 * r, 4096 + 1280 + 128 * r + 128))
    cols.append(np.arange(5632 + 8 * r, 5632 + 8 * r + 8))
    return np.concatenate(cols)


def chunked(v):
    return np.ascontiguousarray(v.reshape(-1, 128).T)


_PROGS = {}


def get_prog(name):
    if name not in _PROGS:
        _PROGS[name] = build_pab() if name == "pab" else build_pc()
    return _PROGS[name]


def run_pab(xfull, i, P, consts, ncores=8):
    nc = get_prog("pab")
    in_maps = []
    for core in range(ncores):
        b, r = core // 2, core % 2
        cols = role_cols(r)
        xbc_cols = np.concatenate([np.arange(512 * r, 512 * r + 512), np.arange(1024 + 128 * r, 1024 + 128 * r + 128),
                                   np.arange(1280 + 128 * r, 1280 + 128 * r + 128)])
        m = dict(consts)
        m["xT"] = np.ascontiguousarray(xfull[b].T)
        m["n1w"] = chunked(P["norm1_w"][i])
        m["w_in"] = np.ascontiguousarray(P["w_in"][i][:, cols])
        m["qkw"] = np.ascontiguousarray(np.stack([P["q_norm_w"][i], P["k_norm_w"][i]], axis=1))
        cwr = P["ssm_conv_w"][i][:, xbc_cols]
        m["cw"] = np.ascontiguousarray(cwr.T.reshape(6, 128, 4).transpose(1, 0, 2))
        m["cb"] = chunked(P["ssm_conv_b"][i][xbc_cols])
        m["hv"] = np.ascontiguousarray(np.stack([P["dt_bias"][i][8 * r:8 * r + 8], P["a_log"][i][8 * r:8 * r + 8],
                                                 P["d_skip"][i][8 * r:8 * r + 8]]))
        m["snw"] = np.ascontiguousarray(P["ssm_norm_w"][i][512 * r:512 * r + 512].reshape(1, 512))
        in_maps.append(m)
    res = run_bass_kernel_spmd(nc, in_maps, core_ids=list(range(ncores)))
    out = []
    for b in range(ncores // 2):
        m0 = np.asarray(res.results[2 * b]["mixT"])
        m1 = np.asarray(res.results[2 * b + 1]["mixT"])
        out.append(np.concatenate([m0[0:512], m1[0:512], m0[512:1024], m1[512:1024]], axis=0))
    return out


def run_pc(xfull, mix, i, P, consts, ncores=8):
    nc = get_prog("pc")
    in_maps = []
    for core in range(ncores):
        b, r = core // 2, core % 2
        xt = np.zeros((D, 2050), np.float32)
        mt = np.zeros((D, 2050), NPBF)
        s0 = r * 2048
        xt[:, 2:] = xfull[b][s0:s0 + 2048].T
        mt[:, 2:] = mix[b][:, s0:s0 + 2048]
        if r == 1:
            xt[:, 0:2] = xfull[b][s0 - 2:s0].T
            mt[:, 0:2] = mix[b][:, s0 - 2:s0]
        m = {"c2048": consts["c2048"]}
        m["xT"] = xt
        m["mT"] = mt
        m["n2w"] = chunked(P["norm2_w"][i])
        m["w_out"] = P["w_out"][i]
        m["w_up"] = P["w_up"][i]
        m["w_dn"] = P["w_down"][i]
        m["fcw"] = np.ascontiguousarray(P["ffn_conv_w"][i].T.reshape(88, 128, 3).transpose(1, 0, 2))
        m["fcb"] = chunked(P["ffn_conv_b"][i])
        in_maps.append(m)
    res = run_bass_kernel_spmd(nc, in_maps, core_ids=list(range(ncores)))
    xn = np.empty_like(xfull)
    for core in range(ncores):
        b, r = core // 2, core % 2
        xn[b, r * 2048:(r + 1) * 2048] = np.asarray(res.results[core]["oT"]).T
    return xn


def kernel(**inputs):
    P = {k: np.asarray(v) for k, v in inputs.items()}
    x = np.ascontiguousarray(P["x"], dtype=np.float32)
    consts = const_tables()
    for i in range(2):
        mix = run_pab(x, i, P, consts)
        x = run_pc(x, mix, i, P, consts)
    return x
```

```python
import contextlib
import numpy as np
import ml_dtypes
import concourse.bass as bass
import concourse.mybir as mybir
from concourse.bass_utils import run_bass_kernel_spmd

F32 = mybir.dt.float32
BF16 = mybir.dt.bfloat16
AF = mybir.ActivationFunctionType
ALU = mybir.AluOpType
AX = mybir.AxisListType
NPBF = ml_dtypes.bfloat16

D = 2048
KC = 16
T = 4096
DFF = 5632
NIN = 5648
EPS = 1e-6
BIG = 30000.0
SCALE = 128.0 ** -0.5


class Buf:
    __slots__ = ("name", "w", "r")

    def __init__(self, name):
        self.name = name
        self.w = None
        self.r = []


class V:
    __slots__ = ("ap", "buf")

    def __init__(self, ap, buf):
        self.ap = ap
        self.buf = buf

    def __getitem__(self, k):
        return V(self.ap[k], self.buf)

    def r(self, pat, **kw):
        return V(self.ap.rearrange(pat, **kw), self.buf)

    def bc(self, shape):
        return V(self.ap.broadcast_to(shape), self.buf)

    def us(self, axis):
        return V(self.ap.unsqueeze(axis), self.buf)


class Tile:
    def __init__(self, t, name):
        self.t = t
        self.buf = Buf(name)

    def __getitem__(self, k):
        return V(self.t[k], self.buf)


class DT:
    def __init__(self, ap, name):
        self.ap = ap
        self.name = name
        self.bufs = {}

    def v(self, key, ap):
        b = self.bufs.get(key)
        if b is None:
            b = Buf(f"{self.name}{key}")
            self.bufs[key] = b
        return V(ap, b)


class Op:
    __slots__ = ("eng", "fn", "kind", "waits", "signal", "eidx", "sigval", "dsem", "dval", "phase")


COMPUTE = ("pe", "act", "dve", "pool")
DMAQ = ("sp", "pool")
ENGS = ("pe", "act", "dve", "pool", "sp")
KD = 6


class Sched:
    def __init__(self, nc, stack):
        self.nc = nc
        self.csem = {e: stack.enter_context(nc.semaphore("c_" + e)) for e in COMPUTE}
        self.dsem = {(q, i): stack.enter_context(nc.semaphore(f"d_{q}{i}")) for q in DMAQ for i in range(KD)}
        self.cnt = {e: 0 for e in COMPUTE}
        self.ndma = {q: 0 for q in DMAQ}
        self.dma_ops = {q: [] for q in DMAQ}
        self.phase = 0
        self.barrier = []
        self._reset()

    def _reset(self):
        self.ops = {e: [] for e in ENGS}
        self.seen = {e: {} for e in ENGS}

    def add(self, eng, fn, reads=(), writes=(), dma=False):
        op = Op()
        op.eng = eng
        op.fn = fn
        op.kind = "d" if dma else "c"
        op.signal = False
        op.waits = []
        op.eidx = len(self.ops[eng])
        op.sigval = None
        op.phase = self.phase
        reads = [x.buf if isinstance(x, (V, Tile)) else x for x in reads]
        writes = [x.buf if isinstance(x, (V, Tile)) else x for x in writes]
        deps = []
        for b in reads:
            if b.w is not None:
                deps.append((b.w, True))
        for b in writes:
            if b.w is not None:
                deps.append((b.w, False))
            for r in b.r:
                deps.append((r, False))
        if dma:
            q = eng
            j = self.ndma[q]
            self.ndma[q] += 1
            op.dsem = (q, j % KD)
            op.dval = 16 * (j // KD + 1)
            if j >= KD:
                deps.append((self.dma_ops[q][j - KD], True))
            self.dma_ops[q].append(op)
        seen = self.seen[eng]
        for p, raw in deps:
            if p is op or p.phase < self.phase:
                continue
            if p.kind == "c":
                if p.eng == eng and (eng == "pe" or not raw):
                    continue
                key = ("c", p.eng)
                if seen.get(key, -1) >= p.eidx:
                    continue
                seen[key] = p.eidx
                p.signal = True
                op.waits.append(p)
            else:
                key = ("d",) + p.dsem
                if seen.get(key, -1) >= p.dval:
                    continue
                seen[key] = p.dval
                op.waits.append(p)
        self.ops[eng].append(op)
        for b in reads:
            b.r.append(op)
        for b in writes:
            b.w = op
            b.r = []
        return op

    def dma(self, q, out, in_, er=(), ew=()):
        o, i = out.ap, in_.ap
        return self.add(q, lambda e: e.dma_start(out=o, in_=i), [in_, *er], [out, *ew], dma=True)

    def mm(self, out, lhsT, rhs, start, stop, er=()):
        o, l, r = out.ap, lhsT.ap, rhs.ap
        return self.add("pe", lambda e: e.matmul(o, l, r, start=start, stop=stop), [lhsT, rhs, *er], [out])

    def tr(self, out, in_, ident):
        o, i, d = out.ap, in_.ap, ident.ap
        return self.add("pe", lambda e: e.transpose(o, i, d), [in_, ident], [out])

    def act(self, out, in_, func, bias=None, scale=None):
        o, i = out.ap, in_.ap
        kw = {}
        rd = [in_]
        if bias is not None:
            if isinstance(bias, V):
                kw["bias"] = bias.ap
                rd.append(bias)
            else:
                kw["bias"] = bias
        if scale is not None:
            if isinstance(scale, V):
                kw["scale"] = scale.ap
                rd.append(scale)
            else:
                kw["scale"] = scale
        return self.add("act", lambda e: e.activation(o, i, func, **kw), rd, [out])

    def tt(self, out, in0, in1, op, eng="dve"):
        o, a, b = out.ap, in0.ap, in1.ap
        return self.add(eng, lambda e: e.tensor_tensor(o, a, b, op), [in0, in1], [out])

    def ts(self, out, in0, s1, s2, op0, op1=None, eng="dve"):
        o, a = out.ap, in0.ap
        rd = [in0]
        if isinstance(s1, V):
            rd.append(s1)
            s1 = s1.ap
        if isinstance(s2, V):
            rd.append(s2)
            s2 = s2.ap
        if op1 is None:
            return self.add(eng, lambda e: e.tensor_scalar(o, a, s1, None, op0), rd, [out])
        return self.add(eng, lambda e: e.tensor_scalar(o, a, s1, s2, op0, op1), rd, [out])

    def stt(self, out, in0, sc, in1, op0, op1):
        o, a, b = out.ap, in0.ap, in1.ap
        rd = [in0, in1]
        if isinstance(sc, V):
            rd.append(sc)
            sc = sc.ap
        return self.add("dve", lambda e: e.scalar_tensor_tensor(o, a, sc, b, op0, op1), rd, [out])

    def cp(self, out, in_, eng="dve"):
        o, i = out.ap, in_.ap
        return self.add(eng, lambda e: e.tensor_copy(o, i), [in_], [out])

    def recip(self, out, in_):
        o, i = out.ap, in_.ap
        return self.add("dve", lambda e: e.reciprocal(o, i), [in_], [out])

    def red(self, out, in_, op=ALU.add):
        o, i = out.ap, in_.ap
        return self.add("dve", lambda e: e.tensor_reduce(o, i, AX.X, op), [in_], [out])

    def max8(self, out, in_):
        o, i = out.ap, in_.ap
        return self.add("dve", lambda e: e.max(o, i), [in_], [out])

    def memset(self, out, val, eng="dve"):
        o = out.ap
        return self.add(eng, lambda e: e.memset(o, val), [], [out])

    def dsem_final(self, q, i):
        n = self.ndma[q]
        return 16 * ((n - 1 - i) // KD + 1) if n > i else 0

    def end_phase(self, final=False):
        nc = self.nc
        for e in COMPUTE:
            last = None
            for op in self.ops[e]:
                if op.kind == "c":
                    last = op
            if last is not None:
                last.signal = True
            for op in self.ops[e]:
                if op.kind == "c" and op.signal:
                    self.cnt[e] += 1
                    op.sigval = self.cnt[e]
        barrier = self.barrier
        csem, dsem = self.csem, self.dsem
        ops = self.ops

        def run(name, e):
            for sem, val in barrier:
                e.wait_ge(sem, val)
            for op in ops[name]:
                for p in op.waits:
                    if p.kind == "c":
                        e.wait_ge(csem[p.eng], p.sigval)
                    else:
                        e.wait_ge(dsem[p.dsem], p.dval)
                inst = op.fn(e)
                if op.kind == "d":
                    inst.then_inc(dsem[op.dsem], 16)
                elif op.signal:
                    inst.then_inc(csem[name], 1)
            if final and name in DMAQ:
                for i in range(KD):
                    v = self.dsem_final(name, i)
                    if v:
                        e.wait_ge(dsem[(name, i)], v)

        with nc.Block() as block:
            @block.sync
            def _(e):
                run("sp", e)

            @block.tensor
            def _(e):
                run("pe", e)

            @block.scalar
            def _(e):
                run("act", e)

            @block.vector
            def _(e):
                run("dve", e)

            @block.gpsimd
            def _(e):
                run("pool", e)
        self.barrier = [(csem[e], self.cnt[e]) for e in COMPUTE if self.cnt[e] > 0]
        for i in range(KD):
            v = self.dsem_final("sp", i)
            if v:
                self.barrier.append((dsem[("sp", i)], v))
        self.phase += 1
        self._reset()


class Ctx:
    def __init__(self):
        self.nc = bass.Bass("TRN2", target_bir_lowering=False)
        self.gstack = contextlib.ExitStack()
        self.S = Sched(self.nc, self.gstack)
        self.pstack = None
        self.n = 0

    def begin(self):
        self.pstack = contextlib.ExitStack()

    def end(self, final=False):
        self.S.end_phase(final)
        self.pstack.close()
        self.pstack = None

    def _stack(self, glob):
        return self.gstack if (glob or self.pstack is None) else self.pstack

    def sb(self, shape, dt, glob=False):
        self.n += 1
        name = f"sb{self.n}"
        return Tile(self._stack(glob).enter_context(self.nc.sbuf_tensor(name, list(shape), dt)), name)

    def ps(self, shape, dt=F32, glob=False):
        self.n += 1
        name = f"ps{self.n}"
        return Tile(self._stack(glob).enter_context(self.nc.psum_tensor(name, list(shape), dt)), name)

    def din(self, name, shape, dt):
        return DT(self.nc.dram_tensor(name, list(shape), dt, kind="ExternalInput").ap(), name)

    def dout(self, name, shape, dt):
        return DT(self.nc.dram_tensor(name, list(shape), dt, kind="ExternalOutput").ap(), name)

    def dscr(self, name, shape, dt):
        return DT(self.nc.dram_tensor(name, list(shape), dt).ap(), name)


class Rot:
    def __init__(self, tiles):
        self.tiles = tiles
        self.i = 0

    def get(self):
        t = self.tiles[self.i % len(self.tiles)]
        self.i += 1
        return t


def const_tables():
    c = {}
    c["ident32"] = np.eye(128, dtype=np.float32)
    c["identb"] = np.eye(128, dtype=np.float32).astype(NPBF)
    c["c128"] = np.full((128, 128), 1.0 / 128, np.float32).astype(NPBF)
    c["c2048"] = np.full((128, 128), 1.0 / 2048, np.float32).astype(NPBF)
    c["onesb"] = np.ones((128, 128), np.float32).astype(NPBF)
    c["ones32"] = np.ones((128, 128), np.float32)
    t = np.arange(128)
    c["tri32"] = (t[:, None] <= t[None, :]).astype(np.float32)
    c["segmask"] = np.where(t[:, None] <= t[None, :], 0.0, -BIG).astype(np.float32)
    neg1 = np.zeros((8, 4, 16), np.float32)
    wtab = np.zeros((8, 4, 16), np.float32)
    for qt in range(8):
        for s in range(4):
            own = 2 * qt + s // 2
            for j in range(16):
                neg1[qt, s, j] = -BIG if j >= own else 0.0
                if j == own:
                    wtab[qt, s, j] = 0.0
                elif j < own:
                    wtab[qt, s, j] = -BIG
                else:
                    wtab[qt, s, j] = -2 * BIG
    c["neg1"] = np.broadcast_to(neg1.reshape(1, 512), (128, 512)).copy()
    c["wtab"] = np.broadcast_to(wtab.reshape(1, 512), (128, 512)).copy()
    trim = np.zeros((128, 4, 512), np.float32)
    for m in range(4):
        for s in range(4):
            if m // 2 != s // 2:
                continue
            blk = trim[:, m, s * 128:(s + 1) * 128]
            if s < m:
                blk[:] = -BIG
            elif s == m:
                blk[:] = np.where(t[:, None] <= t[None, :], 0.0, -BIG)
    c["trim"] = trim.astype(NPBF)
    esel = np.zeros((16, 16, 128), np.float32)
    for j in range(16):
        esel[j, j, :] = 1.0
    c["esel"] = esel.reshape(16, 2048).astype(NPBF)
    selr = np.zeros((8, 8, 128), np.float32)
    for r in range(8):
        selr[r, r, :] = 1.0
    c["selr"] = selr.reshape(8, 1024)
    return c


CONST_SPECS = [("ident32", [128, 128], F32), ("identb", [128, 128], BF16), ("c128", [128, 128], BF16),
               ("c2048", [128, 128], BF16), ("onesb", [128, 128], BF16), ("ones32", [128, 128], F32),
               ("tri32", [128, 128], F32), ("segmask", [128, 128], F32), ("neg1", [128, 512], F32),
               ("wtab", [128, 512], F32), ("trim", [128, 4, 512], BF16), ("esel", [16, 2048], BF16),
               ("selr", [8, 1024], F32)]


def build_fused():
    cx = Ctx()
    S = cx.S
    nc = cx.nc
    xT = cx.din("xT", [D, T], F32)
    n1w = cx.din("n1w", [2, 128, KC], F32)
    n2w = cx.din("n2w", [2, 128, KC], F32)
    w_in = cx.din("w_in", [2, D, NIN], F32)
    w_out = cx.din("w_out", [2, D, D], F32)
    w_up = cx.din("w_up", [2, D, 2 * DFF], F32)
    w_dn = cx.din("w_dn", [2, DFF, D], F32)
    qkw = cx.din("qkw", [2, 128, 2], F32)
    cw = cx.din("cw", [2, 128, 12, 4], F32)
    cb = cx.din("cb", [2, 128, 12], F32)
    hv = cx.din("hv", [2, 3, 16], F32)
    snw = cx.din("snw", [2, 1, 1024], F32)
    fcw = cx.din("fcw", [2, 128, 88, 3], F32)
    fcb = cx.din("fcb", [2, 128, 88], F32)
    oT = cx.dout("oT", [D, T], F32)
    win_s = [cx.dscr(f"win_s{L}", [D, NIN], BF16) for L in range(2)]
    wout_s = [cx.dscr(f"wout_s{L}", [D, D], BF16) for L in range(2)]
    wup_s = [cx.dscr(f"wup_s{L}", [D, 2 * DFF], BF16) for L in range(2)]
    wdn_s = [cx.dscr(f"wdn_s{L}", [DFF, D], BF16) for L in range(2)]
    qk_s = cx.dscr("qk_s", [2048, T], BF16)
    v_s = cx.dscr("v_s", [T, 1024], BF16)
    zs_s = cx.dscr("zs_s", [T, 1024], F32)
    xs_s = cx.dscr("xs_s", [T, 1024], F32)
    b_s = cx.dscr("b_s", [T, 256], BF16)
    bt_s = cx.dscr("bt_s", [256, T], BF16)
    ct_s = cx.dscr("ct_s", [256, T], BF16)
    mix_s = cx.dscr("mix_s", [D, T], BF16)
    x1_s = cx.dscr("x1_s", [D, T], F32)
    x2_s = cx.dscr("x2_s", [D, T], F32)

    cx.begin()
    K = {}
    for nm, shp, dt in CONST_SPECS:
        d = cx.din(nm, shp, dt)
        t = cx.sb(shp, dt, glob=True)
        S.dma("sp", t[:], d.v(0, d.ap))
        K[nm] = t
    dt_all = cx.sb([128, 32, 16], F32, glob=True)
    la_all = cx.sb([128, 32, 16], F32, glob=True)
    kmb = cx.sb([128, 8, 16], BF16, glob=True)
    hvt = cx.sb([128, 3, 16], F32, glob=True)
    a_b = cx.sb([128, 16], F32, glob=True)

    def cast_weight(src_ap, dst, rows, blk=128):
        for i in range(rows // blk):
            op = S.dma("pool", dst.v(i, dst.ap[i * blk:(i + 1) * blk, :]),
                       V(src_ap[i * blk:(i + 1) * blk, :], Buf("ext")))
            op.phase = 1 << 30

    for L in range(2):
        cast_weight(w_in.ap[L], win_s[L], D)
        cast_weight(w_out.ap[L], wout_s[L], D)
        cast_weight(w_up.ap[L], wup_s[L], D)
        cast_weight(w_dn.ap[L], wdn_s[L], DFF)
    cx.end()

    def wload(tile, scr, c0, n, nk):
        ap = scr.ap.rearrange("(k p) c -> p k c", p=128)[:, :, c0:c0 + n]
        er = [scr.v(i, scr.ap) for i in range(nk)]
        S.dma("sp", tile[:, 0:nk, 0:n], V(ap, er[0].buf), er=er[1:])

    def b3(v):
        return v.us(2).bc([128, 8, 64])

    def r3(v):
        return v.r("p (r j) -> p r j", r=8)

    for L in range(2):
        xin = xT if L == 0 else x2_s
        xout = x2_s if L == 0 else oT
        cx.begin()
        n1t = cx.sb([128, KC], F32)
        S.dma("sp", n1t[:], n1w.v(0, n1w.ap[L]))
        qkt = cx.sb([128, 2], F32)
        S.dma("sp", qkt[:], qkw.v(0, qkw.ap[L]))
        cwt = cx.sb([128, 12, 4], F32)
        S.dma("sp", cwt[:], cw.v(0, cw.ap[L]))
        cbt = cx.sb([128, 12], F32)
        S.dma("sp", cbt[:], cb.v(0, cb.ap[L]))
        for i in range(3):
            S.dma("sp", hvt[:, i, :], hv.v(0, hv.ap[L, i:i + 1, :].partition_broadcast(128)))
        S.act(a_b[:], hvt[:, 1, :], AF.Exp)
        S.ts(a_b[:], a_b[:], -1.0, None, ALU.mult)
        kms = cx.sb([128, 8, 16], F32)
        carr = cx.sb([128, 12, 3], F32)
        S.memset(carr[:], 0.0)
        xq = Rot([cx.sb([128, 512], F32) for _ in range(3)])
        sqq = Rot([cx.sb([128, 512], BF16) for _ in range(2)])
        hn = cx.sb([128, KC, 512], BF16)
        wq = Rot([cx.sb([128, KC, 128], BF16) for _ in range(4)])
        wbig = Rot([cx.sb([128, KC, 512], BF16) for _ in range(2)])
        f32q = Rot([cx.sb([128, 512], F32) for _ in range(6)])
        b16q = Rot([cx.sb([128, 512], BF16) for _ in range(4)])
        ubuf = Rot([cx.sb([128, 515], F32) for _ in range(2)])
        rsn = cx.sb([128, 512], F32)
        pA = Rot([cx.ps([128, 512]) for _ in range(4)])
        pB = Rot([cx.ps([128, 512]) for _ in range(3)])
        pss = cx.ps([128, 512])
        W = win_s[L]
        for j in range(8):
            t0 = j * 512
            for k in range(KC):
                xc = xq.get()
                S.dma("sp", xc[:], xin.v((k, j), xin.ap[k * 128:(k + 1) * 128, t0:t0 + 512]))
                sq = sqq.get()
                S.act(sq[:], xc[:], AF.Square)
                S.mm(pss[:], K["c2048"][:], sq[:], k == 0, k == KC - 1)
            S.act(rsn[:], pss[:], AF.Sqrt, bias=EPS, scale=1.0)
            S.recip(rsn[:], rsn[:])
            for k in range(KC):
                xc = xq.get()
                S.dma("sp", xc[:], xin.v((k, j), xin.ap[k * 128:(k + 1) * 128, t0:t0 + 512]))
                S.stt(hn[:, k, :], xc[:], n1t[:, k:k + 1], rsn[:], ALU.mult, ALU.mult)
            for role in range(2):
                for cc in range(8):
                    isk = cc >= 4
                    col0 = (1024 if isk else 0) + 512 * role + (cc % 4) * 128
                    row0 = role * 1024 + (512 if isk else 0) + (cc % 4) * 128
                    w = wq.get()
                    wload(w, W, col0, 128, KC)
                    pa = pA.get()
                    for k in range(KC):
                        S.mm(pa[:], w[:, k, :], hn[:, k, :], k == 0, k == KC - 1)
                    sq = sqq.get()
                    S.act(sq[:], pa[:], AF.Square)
                    pb = pB.get()
                    S.mm(pb[:], K["c128"][:], sq[:], True, True)
                    rs = f32q.get()
                    S.act(rs[:], pb[:], AF.Sqrt, bias=EPS, scale=1.0)
                    S.recip(rs[:], rs[:])
                    qn = b16q.get()
                    wi = 1 if isk else 0
                    S.stt(qn[:], pa[:], qkt[:, wi:wi + 1], rs[:], ALU.mult, ALU.mult)
                    S.dma("sp", qk_s.v((row0, j), qk_s.ap[row0:row0 + 128, t0:t0 + 512]), qn[:])
                    if isk:
                        S.red(kms[:, role * 4 + cc - 4, 2 * j:2 * j + 2], qn[:].r("p (b t) -> p b t", b=2))
                for which in range(2):
                    w = wbig.get()
                    wload(w, W, (2048 if which == 0 else 3072) + 512 * role, 512, KC)
                    for s in range(4):
                        pa = pA.get()
                        for k in range(KC):
                            S.mm(pa[:], hn[:, k, s * 128:(s + 1) * 128], w[:, k, :], k == 0, k == KC - 1)
                        r0 = t0 + s * 128
                        if which == 0:
                            o = b16q.get()
                            S.act(o[:], pa[:], AF.Copy)
                            S.dma("sp", v_s.v((role, j, s), v_s.ap[r0:r0 + 128, 512 * role:512 * role + 512]), o[:])
                        else:
                            o = f32q.get()
                            S.act(o[:], pa[:], AF.Silu)
                            S.dma("sp", zs_s.v((role, j, s), zs_s.ap[r0:r0 + 128, 512 * role:512 * role + 512]), o[:])
                for c in range(6):
                    if c < 4:
                        col0 = 4096 + 512 * role + 128 * c
                    elif c == 4:
                        col0 = 4096 + 1024 + 128 * role
                    else:
                        col0 = 4096 + 1280 + 128 * role
                    ci = role * 6 + c
                    w = wq.get()
                    wload(w, W, col0, 128, KC)
                    pa = pA.get()
                    for k in range(KC):
                        S.mm(pa[:], w[:, k, :], hn[:, k, :], k == 0, k == KC - 1)
                    ub = ubuf.get()
                    S.cp(ub[:, 0:3], carr[:, ci, :])
                    S.act(ub[:, 3:515], pa[:], AF.Copy)
                    S.cp(carr[:, ci, :], ub[:, 512:515])
                    acc = f32q.get()
                    S.ts(acc[:], ub[:, 3:515], cwt[:, ci, 3:4], cbt[:, ci:ci + 1], ALU.mult, ALU.add)
                    for tap in (2, 1, 0):
                        S.stt(acc[:], ub[:, tap:tap + 512], cwt[:, ci, tap:tap + 1], acc[:], ALU.mult, ALU.add)
                    if c < 4:
                        xf = f32q.get()
                        S.act(xf[:], acc[:], AF.Silu)
                        pt = pB.get()
                        for s in range(4):
                            S.tr(pt[:, s * 128:(s + 1) * 128], xf[:, s * 128:(s + 1) * 128], K["ident32"][:])
                        xo = f32q.get()
                        S.cp(xo[:], pt[:])
                        f0 = 512 * role + 128 * c
                        S.dma("sp", xs_s.v((role, j, c), xs_s.ap[t0:t0 + 512, f0:f0 + 128].rearrange("(s p) f -> p s f", p=128)),
                              xo[:].r("p (s f) -> p s f", s=4))
                    elif c == 4:
                        xf = f32q.get()
                        S.act(xf[:], acc[:], AF.Silu)
                        bb = b16q.get()
                        S.cp(bb[:], xf[:])
                        S.dma("sp", bt_s.v((role, j), bt_s.ap[128 * role:128 * role + 128, t0:t0 + 512]), bb[:])
                        pt = pB.get()
                        for s in range(4):
                            S.tr(pt[:, s * 128:(s + 1) * 128], xf[:, s * 128:(s + 1) * 128], K["ident32"][:])
                        bo = b16q.get()
                        S.cp(bo[:], pt[:])
                        S.dma("sp", b_s.v((role, j), b_s.ap[t0:t0 + 512, 128 * role:128 * role + 128].rearrange("(s p) f -> p s f", p=128)),
                              bo[:].r("p (s f) -> p s f", s=4))
                    else:
                        cbf = b16q.get()
                        S.act(cbf[:], acc[:], AF.Silu)
                        S.dma("sp", ct_s.v((role, j), ct_s.ap[128 * role:128 * role + 128, t0:t0 + 512]), cbf[:])
            w = wq.get()
            wload(w, W, 5632, 16, KC)
            for s in range(4):
                pb = pB.get()
                for k in range(KC):
                    S.mm(pb[:, 0:16], hn[:, k, s * 128:(s + 1) * 128], w[:, k, 0:16], k == 0, k == KC - 1)
                ci = j * 4 + s
                S.tt(dt_all[:, ci, :], pb[:, 0:16], hvt[:, 0, :], ALU.add)
                S.act(dt_all[:, ci, :], dt_all[:, ci, :], AF.Exp)
                S.act(dt_all[:, ci, :], dt_all[:, ci, :], AF.Ln, bias=1.0, scale=1.0)
                S.tt(la_all[:, ci, :], dt_all[:, ci, :], a_b[:], ALU.mult)
        S.cp(kmb[:], kms[:])
        cx.end()

        cx.begin()
        kTq = Rot([cx.sb([128, T], BF16) for _ in range(2)])
        vhq = Rot([cx.sb([128, 32, 128], BF16) for _ in range(2)])
        qtq = Rot([cx.sb([128, 512], BF16) for _ in range(3)])
        ptq = Rot([cx.sb([128, 512], BF16) for _ in range(4)])
        f32q = Rot([cx.sb([128, 512], F32) for _ in range(2)])
        b16q = Rot([cx.sb([128, 512], BF16) for _ in range(2)])
        gmq = Rot([cx.sb([128, 4, 16], F32) for _ in range(2)])
        m8q = Rot([cx.sb([128, 4, 8], F32) for _ in range(2)])
        tsq = Rot([cx.sb([128, 4, 16], F32) for _ in range(2)])
        bqq = Rot([cx.sb([128, 4, 16], F32) for _ in range(2)])
        btq_ = Rot([cx.sb([16, 512], BF16) for _ in range(2)])
        pA = Rot([cx.ps([128, 512]) for _ in range(3)])
        pOq = Rot([cx.ps([128, 512]) for _ in range(2)])
        pDq = Rot([cx.ps([128, 512]) for _ in range(2)])
        pG = cx.ps([128, 512])
        for hh in range(8):
            role, h = hh // 4, hh % 4
            qrow = role * 1024 + h * 128
            krow = role * 1024 + 512 + h * 128
            kT = kTq.get()
            kb = [qk_s.v((krow, j), qk_s.ap) for j in range(8)]
            S.dma("sp", kT[:], V(qk_s.ap[krow:krow + 128, :], kb[0].buf), er=kb[1:])
            vh = vhq.get()
            vb = [v_s.v((role, j, s), v_s.ap) for j in range(8) for s in range(4)]
            S.dma("sp", vh[:], V(v_s.ap[:, 512 * role + h * 128:512 * role + (h + 1) * 128].rearrange("(k p) d -> p k d", p=128),
                                 vb[0].buf), er=vb[1:])
            for qt in range(8):
                q0 = qt * 512
                qT = qtq.get()
                S.dma("sp", qT[:], qk_s.v((qrow, qt), qk_s.ap[qrow:qrow + 128, q0:q0 + 512]))
                for s in range(4):
                    S.mm(pG[:, s * 16:(s + 1) * 16], qT[:, s * 128:(s + 1) * 128], kmb[:, hh, :], True, True)
                gm = gmq.get()
                S.tt(gm[:].r("p s j -> p (s j)"), pG[:, 0:64], K["neg1"][:, qt * 64:(qt + 1) * 64], ALU.add)
                m8 = m8q.get()
                for s in range(4):
                    S.max8(m8[:, s, :], gm[:, s, :])
                tsel = tsq.get()
                for s in range(4):
                    S.ts(tsel[:, s, :], gm[:, s, :], m8[:, s, 2:3], None, ALU.is_ge)
                bq = bqq.get()
                S.stt(bq[:].r("p s j -> p (s j)"), tsel[:].r("p s j -> p (s j)"), BIG,
                      K["wtab"][:, qt * 64:(qt + 1) * 64], ALU.mult, ALU.add)
                S.ts(bq[:], bq[:], 0.0, None, ALU.min)
                for s in range(4):
                    S.tr(pG[0:16, s * 128:(s + 1) * 128], bq[:, s, :], K["ident32"][:])
                biasT = btq_.get()
                S.act(biasT[:], pG[0:16, :], AF.Copy)
                nkt = 4 * qt + 4
                pO = pOq.get()
                pDen = pDq.get()
                for kt in range(nkt):
                    jb = kt // 2
                    pS = pA.get()
                    diag = kt >= 4 * qt
                    S.mm(pS[:], kT[:, kt * 128:(kt + 1) * 128], qT[:], True, False)
                    S.mm(pS[:], K["esel"][:, jb * 128:(jb + 1) * 128], biasT[:], False, not diag)
                    if diag:
                        S.mm(pS[:], K["identb"][:], K["trim"][:, kt - 4 * qt, :], False, True)
                    pT = ptq.get()
                    S.act(pT[:], pS[:], AF.Exp, scale=SCALE)
                    S.mm(pO[:], vh[:, kt, :], pT[:], kt == 0, kt == nkt - 1)
                    S.mm(pDen[:], K["onesb"][:], pT[:], kt == 0, kt == nkt - 1)
                rc = f32q.get()
                S.recip(rc[:], pDen[:])
                ao = b16q.get()
                S.tt(ao[:], pO[:], rc[:], ALU.mult)
                mrow = 512 * role + h * 128
                S.dma("sp", mix_s.v((mrow, qt), mix_s.ap[mrow:mrow + 128, q0:q0 + 512]), ao[:])
        cx.end()

        cx.begin()
        snt = cx.sb([128, 1024], F32)
        S.dma("sp", snt[:], snw.v(0, snw.ap[L].partition_broadcast(128)))
        xsq = Rot([cx.sb([128, 512], F32) for _ in range(3)])
        zsq = Rot([cx.sb([128, 512], F32) for _ in range(3)])
        btq = Rot([cx.sb([128, 128], BF16) for _ in range(3)])
        ctq = Rot([cx.sb([128, 128], BF16) for _ in range(3)])
        bmq = Rot([cx.sb([128, 128], BF16) for _ in range(3)])
        f32q = Rot([cx.sb([128, 512], F32) for _ in range(6)])
        acq = Rot([cx.sb([128, 8], F32) for _ in range(2)])
        nacq = Rot([cx.sb([128, 8], F32) for _ in range(2)])
        actq = Rot([cx.sb([8, 128], F32) for _ in range(2)])
        cdq = Rot([cx.sb([128, 8], F32) for _ in range(2)])
        dteq = Rot([cx.sb([128, 8], F32) for _ in range(2)])
        eacq = Rot([cx.sb([128, 8], F32) for _ in range(2)])
        Er = Rot([cx.sb([128, 128], F32) for _ in range(4)])
        Mq = Rot([cx.sb([128, 8, 128], BF16) for _ in range(2)])
        xdfq = Rot([cx.sb([128, 512], F32) for _ in range(2)])
        xdbq = Rot([cx.sb([128, 512], BF16) for _ in range(2)])
        xd2q = Rot([cx.sb([128, 512], BF16) for _ in range(2)])
        ssq = Rot([cx.sb([128, 1], F32) for _ in range(2)])
        ymq = Rot([cx.sb([128, 512], BF16) for _ in range(2)])
        Hs = [cx.sb([128, 512], F32) for _ in range(2)]
        Hbs = [cx.sb([128, 512], BF16) for _ in range(2)]
        pA = Rot([cx.ps([128, 512]) for _ in range(5)])
        pB = Rot([cx.ps([128, 512]) for _ in range(3)])
        for g in range(2):
            S.memset(Hs[g][:], 0.0)
            S.memset(Hbs[g][:], 0.0)
        for c in range(32):
            for g in range(2):
                H, Hb = Hs[g], Hbs[g]
                j, s = c // 4, c % 4
                r0 = c * 128
                xs = xsq.get()
                xb = [xs_s.v((g, j, i), xs_s.ap) for i in range(4)]
                S.dma("sp", xs[:], V(xs_s.ap[r0:r0 + 128, 512 * g:512 * g + 512], xb[0].buf), er=xb[1:])
                zs = zsq.get()
                S.dma("sp", zs[:], zs_s.v((g, j, s), zs_s.ap[r0:r0 + 128, 512 * g:512 * g + 512]))
                bt = btq.get()
                S.dma("sp", bt[:], bt_s.v((g, j), bt_s.ap[128 * g:128 * g + 128, r0:r0 + 128]))
                ct = ctq.get()
                S.dma("sp", ct[:], ct_s.v((g, j), ct_s.ap[128 * g:128 * g + 128, r0:r0 + 128]))
                bm = bmq.get()
                S.dma("sp", bm[:], b_s.v((g, j), b_s.ap[r0:r0 + 128, 128 * g:128 * g + 128]))
                la = la_all[:, c, 8 * g:8 * g + 8]
                dtc = dt_all[:, c, 8 * g:8 * g + 8]
                acum, nacum, acT = acq.get(), nacq.get(), actq.get()
                cdec, dte, eac = cdq.get(), dteq.get(), eacq.get()
                p1 = pB.get()
                S.mm(p1[:, 0:8], K["tri32"][:], la, True, True)
                S.mm(p1[:, 8:16], K["ones32"][:], la, True, True)
                S.cp(acum[:], p1[:, 0:8])
                S.ts(nacum[:], p1[:, 0:8], -1.0, None, ALU.mult)
                S.act(cdec[:], p1[:, 8:16], AF.Exp)
                S.tt(dte[:], p1[:, 8:16], acum[:], ALU.subtract)
                S.act(dte[:], dte[:], AF.Exp)
                S.act(eac[:], acum[:], AF.Exp)
                S.tr(p1[0:8, 128:256], acum[:], K["ident32"][:])
                S.cp(acT[:], p1[0:8, 128:256])
                xdtf, xdtb, xdt2 = xdfq.get(), xdbq.get(), xd2q.get()
                S.tt(r3(xdtf[:]), r3(xs[:]), b3(dtc), ALU.mult)
                S.act(xdtb[:], xdtf[:], AF.Copy)
                S.tt(r3(xdt2[:]), r3(xdtf[:]), b3(dte[:]), ALU.mult)
                pcb = pB.get()
                S.mm(pcb[:, 0:128], bt[:], ct[:], True, True)
                Mall = Mq.get()
                for r in range(8):
                    pseg = pA.get()
                    S.mm(pseg[:, 0:128], K["selr"][:, r * 128:(r + 1) * 128], acT[:], True, False)
                    S.mm(pseg[:, 0:128], K["ident32"][:], K["segmask"][:], False, True)
                    er_ = Er.get()
                    S.act(er_[:], pseg[:, 0:128], AF.Exp, bias=nacum[:, r:r + 1], scale=1.0)
                    S.tt(Mall[:, r, :], pcb[:, 0:128], er_[:], ALU.mult)
                py = pA.get()
                for r in range(8):
                    S.mm(py[:, r * 64:(r + 1) * 64], Mall[:, r, :], xdtb[:, r * 64:(r + 1) * 64], True, True)
                pyo = pA.get()
                S.mm(pyo[:], ct[:], Hb[:], True, True)
                y = f32q.get()
                S.tt(r3(y[:]), r3(pyo[:]), b3(eac[:]), ALU.mult)
                S.tt(y[:], y[:], py[:], ALU.add)
                t2 = f32q.get()
                S.tt(r3(t2[:]), r3(xs[:]), b3(hvt[:, 2, 8 * g:8 * g + 8]), ALU.mult)
                S.tt(y[:], y[:], t2[:], ALU.add)
                pst = pA.get()
                S.mm(pst[:], bm[:], xdt2[:], True, True)
                S.tt(r3(H[:]), r3(H[:]), b3(cdec[:]), ALU.mult)
                S.tt(H[:], H[:], pst[:], ALU.add)
                S.act(Hb[:], H[:], AF.Copy)
                S.tt(y[:], y[:], zs[:], ALU.mult)
                S.tt(t2[:], y[:], y[:], ALU.mult)
                ssum = ssq.get()
                S.red(ssum[:], t2[:])
                S.act(ssum[:], ssum[:], AF.Sqrt, bias=EPS, scale=1.0 / 512)
                S.recip(ssum[:], ssum[:])
                S.stt(y[:], y[:], ssum[:, 0:1], snt[:, 512 * g:512 * g + 512], ALU.mult, ALU.mult)
                pyt = pB.get()
                for f in range(4):
                    S.tr(pyt[:, f * 128:(f + 1) * 128], y[:, f * 128:(f + 1) * 128], K["ident32"][:])
                ym = ymq.get()
                S.act(ym[:], pyt[:], AF.Copy)
                m0 = 1024 + 512 * g
                S.dma("sp", mix_s.v((m0, c), mix_s.ap[m0:m0 + 512, r0:r0 + 128].rearrange("(f p) t -> p f t", p=128)),
                      ym[:].r("p (f t) -> p f t", f=4))
        cx.end()

        cx.begin()
        n2t = cx.sb([128, KC], F32)
        S.dma("sp", n2t[:], n2w.v(0, n2w.ap[L]))
        fcwt = cx.sb([128, 88, 3], F32)
        S.dma("sp", fcwt[:], fcw.v(0, fcw.ap[L]))
        fcbt = cx.sb([128, 88], F32)
        S.dma("sp", fcbt[:], fcb.v(0, fcb.ap[L]))
        ucar = cx.sb([128, 88, 2], F32)
        S.memset(ucar[:], 0.0)
        mixt = cx.sb([128, KC, 512], BF16)
        hn2 = cx.sb([128, KC, 512], BF16)
        gT = cx.sb([128, 44, 512], BF16)
        xq = Rot([cx.sb([128, 512], F32) for _ in range(4)])
        sqq = Rot([cx.sb([128, 512], BF16) for _ in range(2)])
        wq = Rot([cx.sb([128, KC, 128], BF16) for _ in range(5)])
        wdq = Rot([cx.sb([128, 44, 128], BF16) for _ in range(2)])
        ubq = Rot([cx.sb([128, 514], F32) for _ in range(4)])
        accq = Rot([cx.sb([128, 512], F32) for _ in range(6)])
        rsn = cx.sb([128, 512], F32)
        pA = Rot([cx.ps([128, 512]) for _ in range(6)])
        pss = cx.ps([128, 512])
        for ti in range(8):
            c0 = ti * 512
            mb = [mix_s.v(key, mix_s.ap) for key in list(mix_s.bufs.keys())]
            S.dma("sp", mixt[:], V(mix_s.ap[:, c0:c0 + 512].rearrange("(k p) t -> p k t", p=128), mb[0].buf), er=mb[1:])
            for cc in range(KC):
                w = wq.get()
                wload(w, wout_s[L], cc * 128, 128, KC)
                pa = pA.get()
                for k in range(KC):
                    S.mm(pa[:], w[:, k, :], mixt[:, k, :], k == 0, k == KC - 1)
                xc = xq.get()
                S.dma("sp", xc[:], xin.v((cc, ti), xin.ap[cc * 128:(cc + 1) * 128, c0:c0 + 512]))
                S.tt(xc[:], pa[:], xc[:], ALU.add)
                S.dma("sp", x1_s.v((cc, ti), x1_s.ap[cc * 128:(cc + 1) * 128, c0:c0 + 512]), xc[:])
                sq = sqq.get()
                S.act(sq[:], xc[:], AF.Square)
                S.mm(pss[:], K["c2048"][:], sq[:], cc == 0, cc == KC - 1)
            S.act(rsn[:], pss[:], AF.Sqrt, bias=EPS, scale=1.0)
            S.recip(rsn[:], rsn[:])
            for cc in range(KC):
                xc = xq.get()
                S.dma("sp", xc[:], x1_s.v((cc, ti), x1_s.ap[cc * 128:(cc + 1) * 128, c0:c0 + 512]))
                S.stt(hn2[:, cc, :], xc[:], n2t[:, cc:cc + 1], rsn[:], ALU.mult, ALU.mult)
            for c in range(44):
                res = []
                for half in range(2):
                    ch = c + 44 * half
                    w = wq.get()
                    wload(w, wup_s[L], ch * 128, 128, KC)
                    pa = pA.get()
                    for k in range(KC):
                        S.mm(pa[:], w[:, k, :], hn2[:, k, :], k == 0, k == KC - 1)
                    ub = ubq.get()
                    S.cp(ub[:, 0:2], ucar[:, ch, :])
                    S.act(ub[:, 2:514], pa[:], AF.Copy)
                    S.cp(ucar[:, ch, :], ub[:, 512:514])
                    acc = accq.get()
                    S.ts(acc[:], ub[:, 2:514], fcwt[:, ch, 2:3], fcbt[:, ch:ch + 1], ALU.mult, ALU.add)
                    S.stt(acc[:], ub[:, 1:513], fcwt[:, ch, 1:2], acc[:], ALU.mult, ALU.add)
                    S.stt(acc[:], ub[:, 0:512], fcwt[:, ch, 0:1], acc[:], ALU.mult, ALU.add)
                    res.append(acc)
                sg = accq.get()
                S.act(sg[:], res[0][:], AF.Silu)
                S.tt(gT[:, c, :], sg[:], res[1][:], ALU.mult)
            for cc in range(KC):
                wd = wdq.get()
                wload(wd, wdn_s[L], cc * 128, 128, 44)
                pa = pA.get()
                for f in range(44):
                    S.mm(pa[:], wd[:, f, :], gT[:, f, :], f == 0, f == 43)
                xc = xq.get()
                S.dma("sp", xc[:], x1_s.v((cc, ti), x1_s.ap[cc * 128:(cc + 1) * 128, c0:c0 + 512]))
                S.tt(xc[:], pa[:], xc[:], ALU.add)
                S.dma("sp", xout.v((cc, ti), xout.ap[cc * 128:(cc + 1) * 128, c0:c0 + 512]), xc[:])
        cx.end(final=(L == 1))
    cx.gstack.close()
    return nc


def chunked(v):
    return np.ascontiguousarray(v.reshape(-1, 128).T)


_PROG = []
NCORES = 8


def kernel(**inputs):
    P = {k: np.asarray(v) for k, v in inputs.items()}
    x = np.ascontiguousarray(P["x"], dtype=np.float32)
    ncores = NCORES
    if not _PROG:
        _PROG.append(build_fused())
    nc = _PROG[0]
    consts = const_tables()
    shared = dict(consts)
    shared["n1w"] = np.stack([chunked(P["norm1_w"][i]) for i in range(2)])
    shared["n2w"] = np.stack([chunked(P["norm2_w"][i]) for i in range(2)])
    shared["w_in"] = P["w_in"]
    shared["w_out"] = P["w_out"]
    shared["w_up"] = P["w_up"]
    shared["w_dn"] = P["w_down"]
    shared["qkw"] = np.ascontiguousarray(np.stack([P["q_norm_w"], P["k_norm_w"]], axis=2))
    cols = np.concatenate([np.concatenate([np.arange(512 * r, 512 * r + 512), np.arange(1024 + 128 * r, 1152 + 128 * r),
                                           np.arange(1280 + 128 * r, 1408 + 128 * r)]) for r in range(2)])
    cwr = P["ssm_conv_w"][:, :, cols]
    shared["cw"] = np.ascontiguousarray(cwr.transpose(0, 2, 1).reshape(2, 12, 128, 4).transpose(0, 2, 1, 3))
    shared["cb"] = np.stack([chunked(P["ssm_conv_b"][i][cols]) for i in range(2)])
    shared["hv"] = np.ascontiguousarray(np.stack([P["dt_bias"], P["a_log"], P["d_skip"]], axis=1))
    shared["snw"] = np.ascontiguousarray(P["ssm_norm_w"].reshape(2, 1, 1024))
    shared["fcw"] = np.ascontiguousarray(P["ffn_conv_w"].transpose(0, 2, 1).reshape(2, 88, 128, 3).transpose(0, 2, 1, 3))
    shared["fcb"] = np.stack([chunked(P["ffn_conv_b"][i]) for i in range(2)])
    in_maps = []
    for core in range(ncores):
        m = dict(shared)
        m["xT"] = np.ascontiguousarray(x[core // 2].T)
        in_maps.append(m)
    res = run_bass_kernel_spmd(nc, in_maps, core_ids=list(range(ncores)))
    out = np.empty((ncores // 2, T, D), np.float32)
    for core in range(ncores):
        b, r = core // 2, core % 2
        o = np.asarray(res.results[core]["oT"])
        out[b, r * 2048:(r + 1) * 2048] = o[:, r * 2048:(r + 1) * 2048].T
    return out
```

```python
import contextlib
import numpy as np
import ml_dtypes
import concourse.bass as bass
import concourse.mybir as mybir
from concourse.bass_utils import run_bass_kernel_spmd

F32 = mybir.dt.float32
BF16 = mybir.dt.bfloat16
AF = mybir.ActivationFunctionType
ALU = mybir.AluOpType
AX = mybir.AxisListType
NPBF = ml_dtypes.bfloat16

D = 2048
KC = 16
T = 4096
DFF = 5632
NIN = 5648
EPS = 1e-6
BIG = 30000.0
SCALE = 128.0 ** -0.5


class Buf:
    __slots__ = ("name", "w", "r")

    def __init__(self, name):
        self.name = name
        self.w = None
        self.r = []


class V:
    __slots__ = ("ap", "buf")

    def __init__(self, ap, buf):
        self.ap = ap
        self.buf = buf

    def __getitem__(self, k):
        return V(self.ap[k], self.buf)

    def r(self, pat, **kw):
        return V(self.ap.rearrange(pat, **kw), self.buf)

    def bc(self, shape):
        return V(self.ap.broadcast_to(shape), self.buf)

    def us(self, axis):
        return V(self.ap.unsqueeze(axis), self.buf)


class Tile:
    def __init__(self, t, name):
        self.t = t
        self.buf = Buf(name)

    def __getitem__(self, k):
        return V(self.t[k], self.buf)


class DT:
    def __init__(self, ap, name):
        self.ap = ap
        self.name = name
        self.bufs = {}

    def v(self, key, ap):
        b = self.bufs.get(key)
        if b is None:
            b = Buf(f"{self.name}{key}")
            self.bufs[key] = b
        return V(ap, b)


class Op:
    __slots__ = ("eng", "fn", "kind", "waits", "signal", "eidx", "sigval", "dsem", "dval", "phase")


COMPUTE = ("pe", "act", "dve", "pool")
DMAQ = ("sp", "pool")
ENGS = ("pe", "act", "dve", "pool", "sp")
KD = 6


class Sched:
    def __init__(self, nc, stack):
        self.nc = nc
        self.csem = {e: stack.enter_context(nc.semaphore("c_" + e)) for e in COMPUTE}
        self.dsem = {(q, i): stack.enter_context(nc.semaphore(f"d_{q}{i}")) for q in DMAQ for i in range(KD)}
        self.cnt = {e: 0 for e in COMPUTE}
        self.ndma = {q: 0 for q in DMAQ}
        self.dma_ops = {q: [] for q in DMAQ}
        self.phase = 0
        self.barrier = []
        self._reset()

    def _reset(self):
        self.ops = {e: [] for e in ENGS}
        self.seen = {e: {} for e in ENGS}

    def add(self, eng, fn, reads=(), writes=(), dma=False):
        op = Op()
        op.eng = eng
        op.fn = fn
        op.kind = "d" if dma else "c"
        op.signal = False
        op.waits = []
        op.eidx = len(self.ops[eng])
        op.sigval = None
        op.phase = self.phase
        reads = [x.buf if isinstance(x, (V, Tile)) else x for x in reads]
        writes = [x.buf if isinstance(x, (V, Tile)) else x for x in writes]
        deps = []
        for b in reads:
            if b.w is not None:
                deps.append((b.w, True))
        for b in writes:
            if b.w is not None:
                deps.append((b.w, False))
            for r in b.r:
                deps.append((r, False))
        if dma:
            q = eng
            j = self.ndma[q]
            self.ndma[q] += 1
            op.dsem = (q, j % KD)
            op.dval = 16 * (j // KD + 1)
            if j >= KD:
                deps.append((self.dma_ops[q][j - KD], True))
            self.dma_ops[q].append(op)
        seen = self.seen[eng]
        best = {}
        for p, raw in deps:
            if p is op or p.phase < self.phase:
                continue
            if p.kind == "c":
                if p.eng == eng and (eng == "pe" or not raw):
                    continue
                b = best.get(p.eng)
                if b is None or p.eidx > b.eidx:
                    best[p.eng] = p
            else:
                key = ("d",) + p.dsem
                if seen.get(key, -1) >= p.dval:
                    continue
                seen[key] = p.dval
                op.waits.append(p)
        for pe_, p in best.items():
            key = ("c", pe_)
            if seen.get(key, -1) >= p.eidx:
                continue
            seen[key] = p.eidx
            p.signal = True
            op.waits.append(p)
        self.ops[eng].append(op)
        for b in reads:
            b.r.append(op)
        for b in writes:
            b.w = op
            b.r = []
        return op

    def dma(self, q, out, in_, er=(), ew=()):
        o, i = out.ap, in_.ap
        return self.add(q, lambda e: e.dma_start(out=o, in_=i), [in_, *er], [out, *ew], dma=True)

    def mm(self, out, lhsT, rhs, start, stop, er=()):
        o, l, r = out.ap, lhsT.ap, rhs.ap
        return self.add("pe", lambda e: e.matmul(o, l, r, start=start, stop=stop), [lhsT, rhs, *er], [out])

    def tr(self, out, in_, ident):
        o, i, d = out.ap, in_.ap, ident.ap
        return self.add("pe", lambda e: e.transpose(o, i, d), [in_, ident], [out])

    def act(self, out, in_, func, bias=None, scale=None):
        o, i = out.ap, in_.ap
        kw = {}
        rd = [in_]
        if bias is not None:
            if isinstance(bias, V):
                kw["bias"] = bias.ap
                rd.append(bias)
            else:
                kw["bias"] = bias
        if scale is not None:
            if isinstance(scale, V):
                kw["scale"] = scale.ap
                rd.append(scale)
            else:
                kw["scale"] = scale
        return self.add("act", lambda e: e.activation(o, i, func, **kw), rd, [out])

    def tt(self, out, in0, in1, op, eng="dve"):
        o, a, b = out.ap, in0.ap, in1.ap
        return self.add(eng, lambda e: e.tensor_tensor(o, a, b, op), [in0, in1], [out])

    def ts(self, out, in0, s1, s2, op0, op1=None, eng="dve"):
        o, a = out.ap, in0.ap
        rd = [in0]
        if isinstance(s1, V):
            rd.append(s1)
            s1 = s1.ap
        if isinstance(s2, V):
            rd.append(s2)
            s2 = s2.ap
        if op1 is None:
            return self.add(eng, lambda e: e.tensor_scalar(o, a, s1, None, op0), rd, [out])
        return self.add(eng, lambda e: e.tensor_scalar(o, a, s1, s2, op0, op1), rd, [out])

    def stt(self, out, in0, sc, in1, op0, op1):
        o, a, b = out.ap, in0.ap, in1.ap
        rd = [in0, in1]
        if isinstance(sc, V):
            rd.append(sc)
            sc = sc.ap
        return self.add("dve", lambda e: e.scalar_tensor_tensor(o, a, sc, b, op0, op1), rd, [out])

    def cp(self, out, in_, eng="dve"):
        o, i = out.ap, in_.ap
        return self.add(eng, lambda e: e.tensor_copy(o, i), [in_], [out])

    def recip(self, out, in_):
        o, i = out.ap, in_.ap
        return self.add("dve", lambda e: e.reciprocal(o, i), [in_], [out])

    def red(self, out, in_, op=ALU.add):
        o, i = out.ap, in_.ap
        return self.add("dve", lambda e: e.tensor_reduce(o, i, AX.X, op), [in_], [out])

    def max8(self, out, in_):
        o, i = out.ap, in_.ap
        return self.add("dve", lambda e: e.max(o, i), [in_], [out])

    def memset(self, out, val, eng="dve"):
        o = out.ap
        return self.add(eng, lambda e: e.memset(o, val), [], [out])

    def dsem_final(self, q, i):
        n = self.ndma[q]
        return 16 * ((n - 1 - i) // KD + 1) if n > i else 0

    def end_phase(self, final=False, name=None):
        nc = self.nc
        name = name or f"ph{self.phase}"
        for e in COMPUTE:
            last = None
            for op in self.ops[e]:
                if op.kind == "c":
                    last = op
            if last is not None:
                last.signal = True
            for op in self.ops[e]:
                if op.kind == "c" and op.signal:
                    self.cnt[e] += 1
                    op.sigval = self.cnt[e]
        if final:
            print("signal counts", self.cnt, "dmas", self.ndma)
        barrier = self.barrier
        csem, dsem = self.csem, self.dsem
        ops = self.ops

        def run(name, e):
            for sem, val in barrier:
                e.wait_ge(sem, val)
            for op in ops[name]:
                for p in op.waits:
                    if p.kind == "c":
                        e.wait_ge(csem[p.eng], p.sigval)
                    else:
                        e.wait_ge(dsem[p.dsem], p.dval)
                inst = op.fn(e)
                if op.kind == "d":
                    inst.then_inc(dsem[op.dsem], 16)
                elif op.signal:
                    inst.then_inc(csem[name], 1)
            if final and name in DMAQ:
                for i in range(KD):
                    v = self.dsem_final(name, i)
                    if v:
                        e.wait_ge(dsem[(name, i)], v)

        with nc.named_scope(name), nc.Block() as block:
            @block.sync
            def _(e):
                run("sp", e)

            @block.tensor
            def _(e):
                run("pe", e)

            @block.scalar
            def _(e):
                run("act", e)

            @block.vector
            def _(e):
                run("dve", e)

            @block.gpsimd
            def _(e):
                run("pool", e)
        self.barrier = [(csem[e], self.cnt[e]) for e in COMPUTE if self.cnt[e] > 0]
        for i in range(KD):
            v = self.dsem_final("sp", i)
            if v:
                self.barrier.append((dsem[("sp", i)], v))
        self.phase += 1
        self._reset()


class Ctx:
    def __init__(self):
        self.nc = bass.Bass("TRN2", target_bir_lowering=False)
        self.gstack = contextlib.ExitStack()
        self.S = Sched(self.nc, self.gstack)
        self.pstack = None
        self.n = 0

    def begin(self):
        self.pstack = contextlib.ExitStack()

    def end(self, final=False, name=None):
        self.S.end_phase(final, name)
        self.pstack.close()
        self.pstack = None

    def _stack(self, glob):
        return self.gstack if (glob or self.pstack is None) else self.pstack

    def sb(self, shape, dt, glob=False):
        self.n += 1
        name = f"sb{self.n}"
        return Tile(self._stack(glob).enter_context(self.nc.sbuf_tensor(name, list(shape), dt)), name)

    def ps(self, shape, dt=F32, glob=False):
        self.n += 1
        name = f"ps{self.n}"
        return Tile(self._stack(glob).enter_context(self.nc.psum_tensor(name, list(shape), dt)), name)

    def din(self, name, shape, dt):
        return DT(self.nc.dram_tensor(name, list(shape), dt, kind="ExternalInput").ap(), name)

    def dout(self, name, shape, dt):
        return DT(self.nc.dram_tensor(name, list(shape), dt, kind="ExternalOutput").ap(), name)

    def dscr(self, name, shape, dt):
        return DT(self.nc.dram_tensor(name, list(shape), dt).ap(), name)


class Rot:
    def __init__(self, tiles):
        self.tiles = tiles
        self.i = 0

    def get(self):
        t = self.tiles[self.i % len(self.tiles)]
        self.i += 1
        return t


def const_tables():
    c = {}
    c["ident32"] = np.eye(128, dtype=np.float32)
    c["identb"] = np.eye(128, dtype=np.float32).astype(NPBF)
    c["c128"] = np.full((128, 128), 1.0 / 128, np.float32).astype(NPBF)
    c["c2048"] = np.full((128, 128), 1.0 / 2048, np.float32).astype(NPBF)
    c["onesb"] = np.ones((128, 128), np.float32).astype(NPBF)
    c["ones32"] = np.ones((128, 128), np.float32)
    t = np.arange(128)
    c["tri32"] = (t[:, None] <= t[None, :]).astype(np.float32)
    c["segmask"] = np.where(t[:, None] <= t[None, :], 0.0, -BIG).astype(np.float32)
    neg1 = np.zeros((8, 4, 16), np.float32)
    wtab = np.zeros((8, 4, 16), np.float32)
    for qt in range(8):
        for s in range(4):
            own = 2 * qt + s // 2
            for j in range(16):
                neg1[qt, s, j] = -BIG if j >= own else 0.0
                if j == own:
                    wtab[qt, s, j] = 0.0
                elif j < own:
                    wtab[qt, s, j] = -BIG
                else:
                    wtab[qt, s, j] = -2 * BIG
    c["neg1"] = np.broadcast_to(neg1.reshape(1, 512), (128, 512)).copy()
    c["wtab"] = np.broadcast_to(wtab.reshape(1, 512), (128, 512)).copy()
    trim = np.zeros((128, 4, 512), np.float32)
    for m in range(4):
        for s in range(4):
            if m // 2 != s // 2:
                continue
            blk = trim[:, m, s * 128:(s + 1) * 128]
            if s < m:
                blk[:] = -BIG
            elif s == m:
                blk[:] = np.where(t[:, None] <= t[None, :], 0.0, -BIG)
    c["trim"] = trim.astype(NPBF)
    esel = np.zeros((16, 16, 128), np.float32)
    for j in range(16):
        esel[j, j, :] = 1.0
    c["esel"] = esel.reshape(16, 2048).astype(NPBF)
    selr = np.zeros((8, 8, 128), np.float32)
    for r in range(8):
        selr[r, r, :] = 1.0
    c["selr"] = selr.reshape(8, 1024)
    return c


CONST_SPECS = [("ident32", [128, 128], F32), ("identb", [128, 128], BF16), ("c128", [128, 128], BF16),
               ("c2048", [128, 128], BF16), ("onesb", [128, 128], BF16), ("ones32", [128, 128], F32),
               ("tri32", [128, 128], F32), ("segmask", [128, 128], F32), ("neg1", [128, 512], F32),
               ("wtab", [128, 512], F32), ("trim", [128, 4, 512], BF16), ("esel", [16, 2048], BF16),
               ("selr", [8, 1024], F32)]


class Pre:
    def __init__(self, fns, la):
        self.fns = fns
        self.la = la
        self.n = 0
        self.res = {}

    def get(self, i):
        while self.n <= min(i + self.la, len(self.fns) - 1):
            self.res[self.n] = self.fns[self.n]()
            self.n += 1
        return self.res.pop(i)


def build_fused():
    cx = Ctx()
    S = cx.S
    nc = cx.nc
    xT = cx.din("xT", [D, T], F32)
    n1w = cx.din("n1w", [2, 128, KC], F32)
    n2w = cx.din("n2w", [2, 128, KC], F32)
    w_in = cx.din("w_in", [2, D, NIN], F32)
    w_out = cx.din("w_out", [2, D, D], F32)
    w_up = cx.din("w_up", [2, D, 2 * DFF], F32)
    w_dn = cx.din("w_dn", [2, DFF, D], F32)
    qkw = cx.din("qkw", [2, 128, 2], F32)
    cw = cx.din("cw", [2, 128, 12, 4], F32)
    cb = cx.din("cb", [2, 128, 12], F32)
    hv = cx.din("hv", [2, 3, 16], F32)
    snw = cx.din("snw", [2, 1, 1024], F32)
    fcw = cx.din("fcw", [2, 128, 88, 3], F32)
    fcb = cx.din("fcb", [2, 128, 88], F32)
    oT = cx.dout("oT", [D, T], F32)
    win_s = [cx.dscr(f"win_s{L}", [45, 128, KC, 128], BF16) for L in range(2)]
    wout_s = [cx.dscr(f"wout_s{L}", [16, 128, KC, 128], BF16) for L in range(2)]
    wup_s = [cx.dscr(f"wup_s{L}", [88, 128, KC, 128], BF16) for L in range(2)]
    wdn_s = [cx.dscr(f"wdn_s{L}", [16, 128, 44, 128], BF16) for L in range(2)]
    qk_s = cx.dscr("qk_s", [2048, T], BF16)
    v_s = cx.dscr("v_s", [T, 1024], BF16)
    zs_s = cx.dscr("zs_s", [T, 1024], F32)
    xs_s = cx.dscr("xs_s", [T, 1024], F32)
    b_s = cx.dscr("b_s", [T, 256], BF16)
    bt_s = cx.dscr("bt_s", [256, T], BF16)
    ct_s = cx.dscr("ct_s", [256, T], BF16)
    mix_s = cx.dscr("mix_s", [D, T], BF16)
    x1_s = cx.dscr("x1_s", [D, T], F32)
    x2_s = cx.dscr("x2_s", [D, T], F32)

    cx.begin()
    K = {}
    for nm, shp, dt in CONST_SPECS:
        d = cx.din(nm, shp, dt)
        t = cx.sb(shp, dt, glob=True)
        S.dma("sp", t[:], d.v(0, d.ap))
        K[nm] = t
    dt_all = cx.sb([128, 32, 16], F32, glob=True)
    la_all = cx.sb([128, 32, 16], F32, glob=True)
    kmb = cx.sb([128, 8, 16], BF16, glob=True)
    hvt = cx.sb([128, 3, 16], F32, glob=True)
    a_b = cx.sb([128, 16], F32, glob=True)

    def cast_weight(src_ap, dst, rows, ncc):
        for i in range(rows // 128):
            src = src_ap[i * 128:(i + 1) * 128, 0:ncc * 128].rearrange("p (cc c) -> p cc c", c=128)
            op = S.dma("pool", dst.v(i, dst.ap[0:ncc, :, i, :].rearrange("cc p c -> p cc c")), V(src, Buf("ext")))
            op.phase = 1 << 30

    for L in range(2):
        cast_weight(w_in.ap[L], win_s[L], D, 44)
        for i in range(KC):
            op = S.dma("pool", win_s[L].v(i, win_s[L].ap[44, :, i, 0:16]),
                       V(w_in.ap[L][i * 128:(i + 1) * 128, 5632:5648], Buf("ext")))
            op.phase = 1 << 30
        cast_weight(w_out.ap[L], wout_s[L], D, 16)
        cast_weight(w_up.ap[L], wup_s[L], D, 88)
        cast_weight(w_dn.ap[L], wdn_s[L], DFF, 16)
    cx.end()

    def wload(tile, scr, cc, nk, n=128):
        er = [scr.v(i, scr.ap) for i in range(nk)]
        S.dma("sp", tile[:, 0:nk, 0:n], V(scr.ap[cc][:, :, 0:n], er[0].buf), er=er[1:])

    def wload4(tile, scr, cc0, nk):
        er = [scr.v(i, scr.ap) for i in range(nk)]
        S.dma("sp", tile[:], V(scr.ap[cc0:cc0 + 4].rearrange("cc p k c -> p cc k c"), er[0].buf), er=er[1:])

    def b3(v):
        return v.us(2).bc([128, 8, 64])

    def r3(v):
        return v.r("p (r j) -> p r j", r=8)

    for L in range(2):
        xin = xT if L == 0 else x2_s
        xout = x2_s if L == 0 else oT
        cx.begin()
        n1t = cx.sb([128, KC], F32)
        S.dma("sp", n1t[:], n1w.v(0, n1w.ap[L]))
        qkt = cx.sb([128, 2], F32)
        S.dma("sp", qkt[:], qkw.v(0, qkw.ap[L]))
        cwt = cx.sb([128, 12, 4], F32)
        S.dma("sp", cwt[:], cw.v(0, cw.ap[L]))
        cbt = cx.sb([128, 12], F32)
        S.dma("sp", cbt[:], cb.v(0, cb.ap[L]))
        for i in range(3):
            S.dma("sp", hvt[:, i, :], hv.v(0, hv.ap[L, i:i + 1, :].partition_broadcast(128)))
        S.act(a_b[:], hvt[:, 1, :], AF.Exp)
        S.ts(a_b[:], a_b[:], -1.0, None, ALU.mult)
        kms = cx.sb([128, 8, 16], F32)
        carr = cx.sb([128, 12, 3], F32)
        S.memset(carr[:], 0.0)
        xq = Rot([cx.sb([128, 512], F32) for _ in range(3)])
        sqq = Rot([cx.sb([128, 512], BF16) for _ in range(2)])
        hn = cx.sb([128, KC, 512], BF16)
        wq = Rot([cx.sb([128, KC, 128], BF16) for _ in range(4)])
        wbig = Rot([cx.sb([128, 4, KC, 128], BF16) for _ in range(2)])
        f32q = Rot([cx.sb([128, 512], F32) for _ in range(6)])
        b16q = Rot([cx.sb([128, 512], BF16) for _ in range(6)])
        ubuf = Rot([cx.sb([128, 515], F32) for _ in range(2)])
        rsn = cx.sb([128, 512], F32)
        pA = Rot([cx.ps([128, 512]) for _ in range(4)])
        pB = Rot([cx.ps([128, 512]) for _ in range(3)])
        pss = cx.ps([128, 512])
        W = win_s[L]

        def mk_small(cc, n=128):
            def f():
                w = wq.get()
                wload(w, W, cc, KC, n=n)
                return w
            return f

        def mk_big(cc0):
            def f():
                w = wbig.get()
                wload4(w, W, cc0, KC)
                return w
            return f

        fns = []
        for j in range(8):
            for role in range(2):
                for cc in range(8):
                    fns.append(mk_small(((1024 if cc >= 4 else 0) + 512 * role + (cc % 4) * 128) // 128))
                for which in range(2):
                    fns.append(mk_big(((2048 if which == 0 else 3072) + 512 * role) // 128))
                for c in range(6):
                    col0 = 4096 + 512 * role + 128 * c if c < 4 else (4096 + 1024 + 128 * role if c == 4 else 4096 + 1280 + 128 * role)
                    fns.append(mk_small(col0 // 128))
            fns.append(mk_small(44, n=16))
        pre = Pre(fns, 2)
        wi_ = [0]

        def nextw():
            w = pre.get(wi_[0])
            wi_[0] += 1
            return w

        pend = []

        def flush():
            while pend:
                pend.pop(0)()

        for j in range(8):
            t0 = j * 512
            for k in range(KC):
                xc = xq.get()
                S.dma("sp", xc[:], xin.v((k, j), xin.ap[k * 128:(k + 1) * 128, t0:t0 + 512]))
                sq = sqq.get()
                S.act(sq[:], xc[:], AF.Square)
                S.mm(pss[:], K["c2048"][:], sq[:], k == 0, k == KC - 1)
            S.act(rsn[:], pss[:], AF.Sqrt, bias=EPS, scale=1.0)
            S.recip(rsn[:], rsn[:])
            for k in range(KC):
                xc = xq.get()
                S.dma("sp", xc[:], xin.v((k, j), xin.ap[k * 128:(k + 1) * 128, t0:t0 + 512]))
                S.stt(hn[:, k, :], xc[:], n1t[:, k:k + 1], rsn[:], ALU.mult, ALU.mult)
            for role in range(2):
                for cc in range(8):
                    isk = cc >= 4
                    col0 = (1024 if isk else 0) + 512 * role + (cc % 4) * 128
                    row0 = role * 1024 + (512 if isk else 0) + (cc % 4) * 128
                    w = nextw()
                    pa = pA.get()
                    for k in range(KC):
                        S.mm(pa[:], w[:, k, :], hn[:, k, :], k == 0, k == KC - 1)
                    flush()

                    def tail(pa=pa, isk=isk, row0=row0, role=role, cc=cc, j=j, t0=t0):
                        sq = sqq.get()
                        S.act(sq[:], pa[:], AF.Square)
                        pb = pB.get()
                        S.mm(pb[:], K["c128"][:], sq[:], True, True)
                        rs = f32q.get()
                        S.act(rs[:], pb[:], AF.Sqrt, bias=EPS, scale=1.0)
                        S.recip(rs[:], rs[:])
                        qn = b16q.get()
                        wi = 1 if isk else 0
                        S.stt(qn[:], pa[:], qkt[:, wi:wi + 1], rs[:], ALU.mult, ALU.mult)
                        S.dma("sp", qk_s.v((row0, j), qk_s.ap[row0:row0 + 128, t0:t0 + 512]), qn[:])
                        if isk:
                            S.red(kms[:, role * 4 + cc - 4, 2 * j:2 * j + 2], qn[:].r("p (b t) -> p b t", b=2))
                    pend.append(tail)
                for which in range(2):
                    w = nextw()
                    for s in range(4):
                        pa = pA.get()
                        for k in range(KC):
                            S.mm(pa[:].r("p (a b) -> p a b", a=4), hn[:, k, s * 128:(s + 1) * 128], w[:, :, k, :],
                                 k == 0, k == KC - 1)
                        flush()
                        r0 = t0 + s * 128
                        if which == 0:
                            o = b16q.get()
                            S.act(o[:], pa[:], AF.Copy)
                            S.dma("sp", v_s.v((role, j, s), v_s.ap[r0:r0 + 128, 512 * role:512 * role + 512]), o[:])
                        else:
                            o = f32q.get()
                            S.act(o[:], pa[:], AF.Silu)
                            S.dma("sp", zs_s.v((role, j, s), zs_s.ap[r0:r0 + 128, 512 * role:512 * role + 512]), o[:])
                for c in range(6):
                    if c < 4:
                        col0 = 4096 + 512 * role + 128 * c
                    elif c == 4:
                        col0 = 4096 + 1024 + 128 * role
                    else:
                        col0 = 4096 + 1280 + 128 * role
                    ci = role * 6 + c
                    w = nextw()
                    pa = pA.get()
                    for k in range(KC):
                        S.mm(pa[:], w[:, k, :], hn[:, k, :], k == 0, k == KC - 1)
                    flush()

                    def tail(pa=pa, c=c, ci=ci, role=role, j=j, t0=t0):
                        ub = ubuf.get()
                        S.cp(ub[:, 0:3], carr[:, ci, :])
                        S.act(ub[:, 3:515], pa[:], AF.Copy)
                        S.cp(carr[:, ci, :], ub[:, 512:515])
                        acc = f32q.get()
                        S.ts(acc[:], ub[:, 3:515], cwt[:, ci, 3:4], cbt[:, ci:ci + 1], ALU.mult, ALU.add)
                        for tap in (2, 1, 0):
                            S.stt(acc[:], ub[:, tap:tap + 512], cwt[:, ci, tap:tap + 1], acc[:], ALU.mult, ALU.add)
                        if c < 4:
                            xf = f32q.get()
                            S.act(xf[:], acc[:], AF.Silu)
                            pt = pB.get()
                            for s in range(4):
                                S.tr(pt[:, s * 128:(s + 1) * 128], xf[:, s * 128:(s + 1) * 128], K["ident32"][:])
                            xo = f32q.get()
                            S.cp(xo[:], pt[:])
                            f0 = 512 * role + 128 * c
                            S.dma("sp", xs_s.v((role, j, c), xs_s.ap[t0:t0 + 512, f0:f0 + 128].rearrange("(s p) f -> p s f", p=128)),
                                  xo[:].r("p (s f) -> p s f", s=4))
                        elif c == 4:
                            xf = f32q.get()
                            S.act(xf[:], acc[:], AF.Silu)
                            bb = b16q.get()
                            S.cp(bb[:], xf[:])
                            S.dma("sp", bt_s.v((role, j), bt_s.ap[128 * role:128 * role + 128, t0:t0 + 512]), bb[:])
                            pt = pB.get()
                            for s in range(4):
                                S.tr(pt[:, s * 128:(s + 1) * 128], xf[:, s * 128:(s + 1) * 128], K["ident32"][:])
                            bo = b16q.get()
                            S.cp(bo[:], pt[:])
                            S.dma("sp", b_s.v((role, j), b_s.ap[t0:t0 + 512, 128 * role:128 * role + 128].rearrange("(s p) f -> p s f", p=128)),
                                  bo[:].r("p (s f) -> p s f", s=4))
                        else:
                            cbf = b16q.get()
                            S.act(cbf[:], acc[:], AF.Silu)
                            S.dma("sp", ct_s.v((role, j), ct_s.ap[128 * role:128 * role + 128, t0:t0 + 512]), cbf[:])
                    pend.append(tail)
            w = nextw()
            flush()
            for s in range(4):
                pb = pB.get()
                for k in range(KC):
                    S.mm(pb[:, 0:16], hn[:, k, s * 128:(s + 1) * 128], w[:, k, 0:16], k == 0, k == KC - 1)
                ci = j * 4 + s
                S.tt(dt_all[:, ci, :], pb[:, 0:16], hvt[:, 0, :], ALU.add)
                S.act(dt_all[:, ci, :], dt_all[:, ci, :], AF.Exp)
                S.act(dt_all[:, ci, :], dt_all[:, ci, :], AF.Ln, bias=1.0, scale=1.0)
                S.tt(la_all[:, ci, :], dt_all[:, ci, :], a_b[:], ALU.mult)
        S.cp(kmb[:], kms[:])
        cx.end(name=f"A{L}")

        cx.begin()
        kTq = Rot([cx.sb([128, T], BF16) for _ in range(2)])
        vhq = Rot([cx.sb([128, 32, 128], BF16) for _ in range(2)])
        qtq = Rot([cx.sb([128, 512], BF16) for _ in range(3)])
        ptq = Rot([cx.sb([128, 512], BF16) for _ in range(4)])
        f32q = Rot([cx.sb([128, 512], F32) for _ in range(2)])
        b16q = Rot([cx.sb([128, 512], BF16) for _ in range(2)])
        gmq = Rot([cx.sb([128, 4, 16], F32) for _ in range(2)])
        m8q = Rot([cx.sb([128, 4, 8], F32) for _ in range(2)])
        tsq = Rot([cx.sb([128, 4, 16], F32) for _ in range(2)])
        bqq = Rot([cx.sb([128, 4, 16], F32) for _ in range(2)])
        btq_ = Rot([cx.sb([16, 512], BF16) for _ in range(2)])
        pA = Rot([cx.ps([128, 512]) for _ in range(3)])
        pOq = Rot([cx.ps([128, 512]) for _ in range(2)])
        pDq = Rot([cx.ps([128, 512]) for _ in range(2)])
        pG = cx.ps([128, 512])
        heads = {}

        def load_head(hh):
            role, h = hh // 4, hh % 4
            krow = role * 1024 + 512 + h * 128
            kT = kTq.get()
            kb = [qk_s.v((krow, j), qk_s.ap) for j in range(8)]
            S.dma("sp", kT[:], V(qk_s.ap[krow:krow + 128, :], kb[0].buf), er=kb[1:])
            vh = vhq.get()
            vb = [v_s.v((role, j, s), v_s.ap) for j in range(8) for s in range(4)]
            S.dma("sp", vh[:], V(v_s.ap[:, 512 * role + h * 128:512 * role + (h + 1) * 128].rearrange("(k p) d -> p k d", p=128),
                                 vb[0].buf), er=vb[1:])
            heads[hh] = (kT, vh)

        def prep_a(it):
            hh, qt = it
            role, h = hh // 4, hh % 4
            qrow = role * 1024 + h * 128
            q0 = qt * 512
            qT = qtq.get()
            S.dma("sp", qT[:], qk_s.v((qrow, qt), qk_s.ap[qrow:qrow + 128, q0:q0 + 512]))
            for s in range(4):
                S.mm(pG[:, s * 16:(s + 1) * 16], qT[:, s * 128:(s + 1) * 128], kmb[:, hh, :], True, True)
            gm = gmq.get()
            S.tt(gm[:].r("p s j -> p (s j)"), pG[:, 0:64], K["neg1"][:, qt * 64:(qt + 1) * 64], ALU.add)
            m8 = m8q.get()
            for s in range(4):
                S.max8(m8[:, s, :], gm[:, s, :])
            tsel = tsq.get()
            for s in range(4):
                S.ts(tsel[:, s, :], gm[:, s, :], m8[:, s, 2:3], None, ALU.is_ge)
            bq = bqq.get()
            S.stt(bq[:].r("p s j -> p (s j)"), tsel[:].r("p s j -> p (s j)"), BIG,
                  K["wtab"][:, qt * 64:(qt + 1) * 64], ALU.mult, ALU.add)
            S.ts(bq[:], bq[:], 0.0, None, ALU.min)
            return qT, bq

        def prep_b(bq):
            for s in range(4):
                S.tr(pG[0:16, s * 128:(s + 1) * 128], bq[:, s, :], K["ident32"][:])
            biasT = btq_.get()
            S.act(biasT[:], pG[0:16, :], AF.Copy)
            return biasT

        items = [(hh, qt) for hh in range(8) for qt in range(8)]
        load_head(0)
        qT, bq = prep_a(items[0])
        biasT = prep_b(bq)
        for n, (hh, qt) in enumerate(items):
            role, h = hh // 4, hh % 4
            q0 = qt * 512
            kT, vh = heads[hh]
            nxt = items[n + 1] if n + 1 < len(items) else None
            if nxt is not None and nxt[0] != hh:
                load_head(nxt[0])
            nkt = 4 * qt + 4
            pO = pOq.get()
            pDen = pDq.get()

            def issue_s(kt, qT=qT, biasT=biasT, kT=kT, qt=qt):
                jb = kt // 2
                pS = pA.get()
                diag = kt >= 4 * qt
                S.mm(pS[:], kT[:, kt * 128:(kt + 1) * 128], qT[:], True, False)
                S.mm(pS[:], K["esel"][:, jb * 128:(jb + 1) * 128], biasT[:], False, not diag)
                if diag:
                    S.mm(pS[:], K["identb"][:], K["trim"][:, kt - 4 * qt, :], False, True)
                return pS

            pend = issue_s(0)
            nprep = None
            for kt in range(nkt):
                if kt == 0 and nxt is not None:
                    nprep = prep_a(nxt)
                pn = issue_s(kt + 1) if kt + 1 < nkt else None
                pT = ptq.get()
                S.act(pT[:], pend[:], AF.Exp, scale=SCALE)
                S.mm(pO[:], vh[:, kt, :], pT[:], kt == 0, kt == nkt - 1)
                S.mm(pDen[:], K["onesb"][:], pT[:], kt == 0, kt == nkt - 1)
                pend = pn
            if nprep is not None:
                nbias = prep_b(nprep[1])
            rc = f32q.get()
            S.recip(rc[:], pDen[:])
            ao = b16q.get()
            S.tt(ao[:], pO[:], rc[:], ALU.mult)
            mrow = 512 * role + h * 128
            S.dma("sp", mix_s.v((mrow, qt), mix_s.ap[mrow:mrow + 128, q0:q0 + 512]), ao[:])
            if nprep is not None:
                qT, biasT = nprep[0], nbias
        cx.end(name=f"B1_{L}")

        cx.begin()
        snt = cx.sb([128, 1024], F32)
        S.dma("sp", snt[:], snw.v(0, snw.ap[L].partition_broadcast(128)))
        xsq = Rot([cx.sb([128, 512], F32) for _ in range(3)])
        zsq = Rot([cx.sb([128, 512], F32) for _ in range(3)])
        btq = Rot([cx.sb([128, 128], BF16) for _ in range(3)])
        ctq = Rot([cx.sb([128, 128], BF16) for _ in range(3)])
        bmq = Rot([cx.sb([128, 128], BF16) for _ in range(3)])
        f32q = Rot([cx.sb([128, 512], F32) for _ in range(6)])
        acq = Rot([cx.sb([128, 8], F32) for _ in range(2)])
        nacq = Rot([cx.sb([128, 8], F32) for _ in range(2)])
        actq = Rot([cx.sb([8, 128], F32) for _ in range(2)])
        cdq = Rot([cx.sb([128, 8], F32) for _ in range(2)])
        dteq = Rot([cx.sb([128, 8], F32) for _ in range(2)])
        eacq = Rot([cx.sb([128, 8], F32) for _ in range(2)])
        Er = Rot([cx.sb([128, 128], F32) for _ in range(4)])
        Mq = Rot([cx.sb([128, 8, 128], BF16) for _ in range(2)])
        xdfq = Rot([cx.sb([128, 512], F32) for _ in range(2)])
        xdbq = Rot([cx.sb([128, 512], BF16) for _ in range(2)])
        xd2q = Rot([cx.sb([128, 512], BF16) for _ in range(2)])
        ssq = Rot([cx.sb([128, 1], F32) for _ in range(2)])
        ymq = Rot([cx.sb([128, 512], BF16) for _ in range(2)])
        Hs = [cx.sb([128, 512], F32) for _ in range(2)]
        Hbs = [cx.sb([128, 512], BF16) for _ in range(2)]
        pA = Rot([cx.ps([128, 512]) for _ in range(5)])
        pB = Rot([cx.ps([128, 512]) for _ in range(3)])
        for g in range(2):
            S.memset(Hs[g][:], 0.0)
            S.memset(Hbs[g][:], 0.0)
        def mk_ld(c, g):
            def f():
                j, s = c // 4, c % 4
                r0 = c * 128
                xs = xsq.get()
                xb = [xs_s.v((g, j, i), xs_s.ap) for i in range(4)]
                S.dma("sp", xs[:], V(xs_s.ap[r0:r0 + 128, 512 * g:512 * g + 512], xb[0].buf), er=xb[1:])
                zs = zsq.get()
                S.dma("sp", zs[:], zs_s.v((g, j, s), zs_s.ap[r0:r0 + 128, 512 * g:512 * g + 512]))
                bt = btq.get()
                S.dma("sp", bt[:], bt_s.v((g, j), bt_s.ap[128 * g:128 * g + 128, r0:r0 + 128]))
                ct = ctq.get()
                S.dma("sp", ct[:], ct_s.v((g, j), ct_s.ap[128 * g:128 * g + 128, r0:r0 + 128]))
                bm = bmq.get()
                S.dma("sp", bm[:], b_s.v((g, j), b_s.ap[r0:r0 + 128, 128 * g:128 * g + 128]))
                return xs, zs, bt, ct, bm
            return f

        pre = Pre([mk_ld(c, g) for c in range(32) for g in range(2)], 1)
        for c in range(32):
            for g in range(2):
                H, Hb = Hs[g], Hbs[g]
                j, s = c // 4, c % 4
                r0 = c * 128
                xs, zs, bt, ct, bm = pre.get(c * 2 + g)
                la = la_all[:, c, 8 * g:8 * g + 8]
                dtc = dt_all[:, c, 8 * g:8 * g + 8]
                acum, nacum, acT = acq.get(), nacq.get(), actq.get()
                cdec, dte, eac = cdq.get(), dteq.get(), eacq.get()
                p1 = pB.get()
                S.mm(p1[:, 0:8], K["tri32"][:], la, True, True)
                S.mm(p1[:, 8:16], K["ones32"][:], la, True, True)
                S.cp(acum[:], p1[:, 0:8])
                S.ts(nacum[:], p1[:, 0:8], -1.0, None, ALU.mult)
                S.act(cdec[:], p1[:, 8:16], AF.Exp)
                S.tt(dte[:], p1[:, 8:16], acum[:], ALU.subtract)
                S.act(dte[:], dte[:], AF.Exp)
                S.act(eac[:], acum[:], AF.Exp)
                S.tr(p1[0:8, 128:256], acum[:], K["ident32"][:])
                S.cp(acT[:], p1[0:8, 128:256])
                xdtf, xdtb, xdt2 = xdfq.get(), xdbq.get(), xd2q.get()
                S.tt(r3(xdtf[:]), r3(xs[:]), b3(dtc), ALU.mult)
                S.act(xdtb[:], xdtf[:], AF.Copy)
                S.tt(r3(xdt2[:]), r3(xdtf[:]), b3(dte[:]), ALU.mult)
                pcb = pB.get()
                S.mm(pcb[:, 0:128], bt[:], ct[:], True, True)
                Mall = Mq.get()
                for r in range(8):
                    pseg = pA.get()
                    S.mm(pseg[:, 0:128], K["selr"][:, r * 128:(r + 1) * 128], acT[:], True, False)
                    S.mm(pseg[:, 0:128], K["ident32"][:], K["segmask"][:], False, True)
                    er_ = Er.get()
                    S.act(er_[:], pseg[:, 0:128], AF.Exp, bias=nacum[:, r:r + 1], scale=1.0)
                    S.tt(Mall[:, r, :], pcb[:, 0:128], er_[:], ALU.mult)
                py = pA.get()
                for r in range(8):
                    S.mm(py[:, r * 64:(r + 1) * 64], Mall[:, r, :], xdtb[:, r * 64:(r + 1) * 64], True, True)
                pyo = pA.get()
                S.mm(pyo[:], ct[:], Hb[:], True, True)
                y = f32q.get()
                S.tt(r3(y[:]), r3(pyo[:]), b3(eac[:]), ALU.mult)
                S.tt(y[:], y[:], py[:], ALU.add)
                t2 = f32q.get()
                S.tt(r3(t2[:]), r3(xs[:]), b3(hvt[:, 2, 8 * g:8 * g + 8]), ALU.mult)
                S.tt(y[:], y[:], t2[:], ALU.add)
                pst = pA.get()
                S.mm(pst[:], bm[:], xdt2[:], True, True)
                S.tt(r3(H[:]), r3(H[:]), b3(cdec[:]), ALU.mult)
                S.tt(H[:], H[:], pst[:], ALU.add)
                S.act(Hb[:], H[:], AF.Copy)
                S.tt(y[:], y[:], zs[:], ALU.mult)
                S.tt(t2[:], y[:], y[:], ALU.mult)
                ssum = ssq.get()
                S.red(ssum[:], t2[:])
                S.act(ssum[:], ssum[:], AF.Sqrt, bias=EPS, scale=1.0 / 512)
                S.recip(ssum[:], ssum[:])
                S.stt(y[:], y[:], ssum[:, 0:1], snt[:, 512 * g:512 * g + 512], ALU.mult, ALU.mult)
                pyt = pB.get()
                for f in range(4):
                    S.tr(pyt[:, f * 128:(f + 1) * 128], y[:, f * 128:(f + 1) * 128], K["ident32"][:])
                ym = ymq.get()
                S.act(ym[:], pyt[:], AF.Copy)
                m0 = 1024 + 512 * g
                S.dma("sp", mix_s.v((m0, c), mix_s.ap[m0:m0 + 512, r0:r0 + 128].rearrange("(f p) t -> p f t", p=128)),
                      ym[:].r("p (f t) -> p f t", f=4))
        cx.end(name=f"B2_{L}")

        cx.begin()
        n2t = cx.sb([128, KC], F32)
        S.dma("sp", n2t[:], n2w.v(0, n2w.ap[L]))
        fcwt = cx.sb([128, 88, 3], F32)
        S.dma("sp", fcwt[:], fcw.v(0, fcw.ap[L]))
        fcbt = cx.sb([128, 88], F32)
        S.dma("sp", fcbt[:], fcb.v(0, fcb.ap[L]))
        ucar = cx.sb([128, 88, 2], F32)
        S.memset(ucar[:], 0.0)
        mixt = cx.sb([128, KC, 512], BF16)
        hn2 = cx.sb([128, KC, 512], BF16)
        gT = cx.sb([128, 44, 512], BF16)
        xq = Rot([cx.sb([128, 512], F32) for _ in range(3)])
        sqq = Rot([cx.sb([128, 512], BF16) for _ in range(2)])
        wq = Rot([cx.sb([128, KC, 128], BF16) for _ in range(4)])
        wdq = Rot([cx.sb([128, 44, 128], BF16) for _ in range(3)])
        ubq = Rot([cx.sb([128, 514], F32) for _ in range(3)])
        accq = Rot([cx.sb([128, 512], F32) for _ in range(5)])

        def mk_w(scr, cc):
            def f():
                w = wq.get()
                wload(w, scr, cc, KC)
                return w
            return f

        def mk_wd(cc):
            def f():
                w = wdq.get()
                wload(w, wdn_s[L], cc, 44)
                return w
            return f

        fns = []
        for ti in range(8):
            fns += [mk_w(wout_s[L], cc) for cc in range(KC)]
            for c in range(44):
                fns += [mk_w(wup_s[L], c), mk_w(wup_s[L], c + 44)]
            fns += [mk_wd(cc) for cc in range(KC)]
        pre = Pre(fns, 2)
        wi_ = [0]

        def nextw():
            w = pre.get(wi_[0])
            wi_[0] += 1
            return w

        rsn = cx.sb([128, 512], F32)
        pA = Rot([cx.ps([128, 512]) for _ in range(6)])
        pss = cx.ps([128, 512])
        pendc = []

        def flushc():
            while pendc:
                pendc.pop(0)()

        for ti in range(8):
            c0 = ti * 512
            mb = [mix_s.v(key, mix_s.ap) for key in list(mix_s.bufs.keys())]
            S.dma("sp", mixt[:], V(mix_s.ap[:, c0:c0 + 512].rearrange("(k p) t -> p k t", p=128), mb[0].buf), er=mb[1:])
            for cc in range(KC):
                w = nextw()
                pa = pA.get()
                for k in range(KC):
                    S.mm(pa[:], w[:, k, :], mixt[:, k, :], k == 0, k == KC - 1)
                flushc()

                def tail(pa=pa, cc=cc, ti=ti, c0=c0):
                    xc = xq.get()
                    S.dma("sp", xc[:], xin.v((cc, ti), xin.ap[cc * 128:(cc + 1) * 128, c0:c0 + 512]))
                    S.tt(xc[:], pa[:], xc[:], ALU.add)
                    S.dma("sp", x1_s.v((cc, ti), x1_s.ap[cc * 128:(cc + 1) * 128, c0:c0 + 512]), xc[:])
                    sq = sqq.get()
                    S.act(sq[:], xc[:], AF.Square)
                    S.mm(pss[:], K["c2048"][:], sq[:], cc == 0, cc == KC - 1)
                pendc.append(tail)
            flushc()
            S.act(rsn[:], pss[:], AF.Sqrt, bias=EPS, scale=1.0)
            S.recip(rsn[:], rsn[:])
            for cc in range(KC):
                xc = xq.get()
                S.dma("sp", xc[:], x1_s.v((cc, ti), x1_s.ap[cc * 128:(cc + 1) * 128, c0:c0 + 512]))
                S.stt(hn2[:, cc, :], xc[:], n2t[:, cc:cc + 1], rsn[:], ALU.mult, ALU.mult)
            for c in range(44):
                res = []
                for half in range(2):
                    ch = c + 44 * half
                    w = nextw()
                    pa = pA.get()
                    for k in range(KC):
                        S.mm(pa[:], w[:, k, :], hn2[:, k, :], k == 0, k == KC - 1)
                    ub = ubq.get()
                    S.cp(ub[:, 0:2], ucar[:, ch, :])
                    S.act(ub[:, 2:514], pa[:], AF.Copy)
                    S.cp(ucar[:, ch, :], ub[:, 512:514])
                    acc = accq.get()
                    S.ts(acc[:], ub[:, 2:514], fcwt[:, ch, 2:3], fcbt[:, ch:ch + 1], ALU.mult, ALU.add)
                    S.stt(acc[:], ub[:, 1:513], fcwt[:, ch, 1:2], acc[:], ALU.mult, ALU.add)
                    S.stt(acc[:], ub[:, 0:512], fcwt[:, ch, 0:1], acc[:], ALU.mult, ALU.add)
                    res.append(acc)
                sg = accq.get()
                S.act(sg[:], res[0][:], AF.Silu)
                S.tt(gT[:, c, :], sg[:], res[1][:], ALU.mult)
            for cc in range(KC):
                wd = nextw()
                pa = pA.get()
                for f in range(44):
                    S.mm(pa[:], wd[:, f, :], gT[:, f, :], f == 0, f == 43)
                xc = xq.get()
                S.dma("sp", xc[:], x1_s.v((cc, ti), x1_s.ap[cc * 128:(cc + 1) * 128, c0:c0 + 512]))
                S.tt(xc[:], pa[:], xc[:], ALU.add)
                S.dma("sp", xout.v((cc, ti), xout.ap[cc * 128:(cc + 1) * 128, c0:c0 + 512]), xc[:])
        cx.end(final=(L == 1), name=f"C{L}")
    cx.gstack.close()
    return nc


def chunked(v):
    return np.ascontiguousarray(v.reshape(-1, 128).T)


_PROG = []
NCORES = 8


def kernel(**inputs):
    P = {k: np.asarray(v) for k, v in inputs.items()}
    x = np.ascontiguousarray(P["x"], dtype=np.float32)
    ncores = NCORES
    if not _PROG:
        _PROG.append(build_fused())
    nc = _PROG[0]
    consts = const_tables()
    shared = dict(consts)
    shared["n1w"] = np.stack([chunked(P["norm1_w"][i]) for i in range(2)])
    shared["n2w"] = np.stack([chunked(P["norm2_w"][i]) for i in range(2)])
    shared["w_in"] = P["w_in"]
    shared["w_out"] = P["w_out"]
    shared["w_up"] = P["w_up"]
    shared["w_dn"] = P["w_down"]
    shared["qkw"] = np.ascontiguousarray(np.stack([P["q_norm_w"], P["k_norm_w"]], axis=2))
    cols = np.concatenate([np.concatenate([np.arange(512 * r, 512 * r + 512), np.arange(1024 + 128 * r, 1152 + 128 * r),
                                           np.arange(1280 + 128 * r, 1408 + 128 * r)]) for r in range(2)])
    cwr = P["ssm_conv_w"][:, :, cols]
    shared["cw"] = np.ascontiguousarray(cwr.transpose(0, 2, 1).reshape(2, 12, 128, 4).transpose(0, 2, 1, 3))
    shared["cb"] = np.stack([chunked(P["ssm_conv_b"][i][cols]) for i in range(2)])
    shared["hv"] = np.ascontiguousarray(np.stack([P["dt_bias"], P["a_log"], P["d_skip"]], axis=1))
    shared["snw"] = np.ascontiguousarray(P["ssm_norm_w"].reshape(2, 1, 1024))
    shared["fcw"] = np.ascontiguousarray(P["ffn_conv_w"].transpose(0, 2, 1).reshape(2, 88, 128, 3).transpose(0, 2, 1, 3))
    shared["fcb"] = np.stack([chunked(P["ffn_conv_b"][i]) for i in range(2)])
    in_maps = []
    for core in range(ncores):
        m = dict(shared)
        m["xT"] = np.ascontiguousarray(x[core // 2].T)
        in_maps.append(m)
    res = run_bass_kernel_spmd(nc, in_maps, core_ids=list(range(ncores)))
    out = np.empty((ncores // 2, T, D), np.float32)
    for core in range(ncores):
        b, r = core // 2, core % 2
        o = np.asarray(res.results[core]["oT"])
        out[b, r * 2048:(r + 1) * 2048] = o[:, r * 2048:(r + 1) * 2048].T
    return out
```
